# Optimizing a Trainium2 kernel written in Bass

```python
import jax, jax.numpy as jnp
from jax import lax
import numpy as np

D_MODEL = 1024
BATCH = 8
SEQ = 8192
DEPTH = 2

GRID_W = 64
CTX_LEN = 256
HEAD_DIM = 128
N_Q_HEADS = 8
N_KV_HEADS = 2
Q_PER_KV = N_Q_HEADS // N_KV_HEADS
ATTN_WIDTH = N_Q_HEADS * HEAD_DIM
KV_WIDTH = N_KV_HEADS * HEAD_DIM
ROPE_THETA = 10000.0
ROPE_AXIS_DIM = HEAD_DIM // 2
ROPE_FREQS = ROPE_AXIS_DIM // 2
Q_BLOCK = 128
SC_WIDTH = D_MODEL
SC_KERNEL = 3
CF_WIDTH = D_MODEL
CF_KERNEL = 31
N_BRANCHES = 3
D_FF = 4 * D_MODEL
NORM_EPS = 1e-6
LN_EPS = 1e-5
OFF_Q = 0
OFF_K = OFF_Q + ATTN_WIDTH
OFF_V = OFF_K + KV_WIDTH
OFF_SC = OFF_V + KV_WIDTH
OFF_CF = OFF_SC + 3 * SC_WIDTH
OFF_GATE = OFF_CF + 2 * CF_WIDTH
D_IN = OFF_GATE + N_BRANCHES * D_MODEL

kernel_name = "hybrid_gated_attn_shortconv_conformer_dit"


def rmsnorm(x, g):
    xf = x.astype(jnp.float32)
    y = xf * lax.rsqrt(jnp.mean(xf * xf, axis=-1, keepdims=True) + NORM_EPS)
    return (y * g.astype(jnp.float32)).astype(x.dtype)


def layernorm(x, g, b):
    xf = x.astype(jnp.float32)
    mu = jnp.mean(xf, axis=-1, keepdims=True)
    xc = xf - mu
    y = xc * lax.rsqrt(jnp.mean(xc * xc, axis=-1, keepdims=True) + LN_EPS)
    return (y * g.astype(jnp.float32) + b.astype(jnp.float32)).astype(x.dtype)


def modulation(cond, w, b):
    m = jax.nn.silu(cond) @ w + b
    return jnp.split(m[:, None, :], 6, axis=-1)


def axial_rope_tables(n_tokens):
    rows = n_tokens // GRID_W
    row = jnp.repeat(jnp.arange(rows), GRID_W)
    col = jnp.tile(jnp.arange(GRID_W), rows)
    pos = jnp.stack([row, col], axis=-1).astype(jnp.float32)
    inv_freq = ROPE_THETA ** (-jnp.arange(ROPE_FREQS, dtype=jnp.float32) * 2.0 / ROPE_AXIS_DIM)
    ang = pos[:, :, None] * inv_freq
    return jnp.cos(ang), jnp.sin(ang)


def apply_rope(x, cos, sin):
    B, S, H, _ = x.shape
    xr = x.reshape(B, S, H, 2, 2, ROPE_FREQS).astype(jnp.float32)
    xa, xb = xr[..., 0, :], xr[..., 1, :]
    c = cos[None, :, None]
    s = sin[None, :, None]
    out = jnp.stack([xa * c - xb * s, xb * c + xa * s], axis=-2)
    return out.reshape(x.shape).astype(x.dtype)


def dwconv(x, w):
    return lax.conv_general_dilated(
        x, w[:, None, :], window_strides=(1,), padding='SAME',
        dimension_numbers=('NWC', 'WIO', 'NWC'), feature_group_count=x.shape[-1])


def heads(x, n):
    return x.reshape(*x.shape[:-1], n, HEAD_DIM)


def query_heads(p, q_gain):
    return rmsnorm(heads(p[..., OFF_Q:OFF_K], N_Q_HEADS), q_gain)


def keys_values(p_kv, k_gain):
    k = rmsnorm(heads(p_kv[..., :KV_WIDTH], N_KV_HEADS), k_gain)
    v = heads(p_kv[..., KV_WIDTH:], N_KV_HEADS)
    return k, v


def attend(q, k, v):
    s = jnp.einsum('bqhgd,bkhd->bhgqk', q, k).astype(jnp.float32) * (HEAD_DIM ** -0.5)
    p = jax.nn.softmax(s, axis=-1).astype(v.dtype)
    return jnp.einsum('bhgqk,bkhd->bqhgd', p, v)


def latent_attention(q, k, v, k_ctx, v_ctx):
    B, S = q.shape[:2]
    k_all = jnp.concatenate([k_ctx, k], axis=1)
    v_all = jnp.concatenate([v_ctx, v], axis=1)
    n_blk = S // Q_BLOCK
    qb = q.reshape(B, n_blk, Q_BLOCK, N_KV_HEADS, Q_PER_KV, HEAD_DIM).transpose(1, 0, 2, 3, 4, 5)
    out = lax.map(lambda qi: attend(qi, k_all, v_all), qb)
    return out.transpose(1, 0, 2, 3, 4, 5).reshape(B, S, ATTN_WIDTH)


def context_attention(q, k, v):
    B, L = q.shape[:2]
    qg = q.reshape(B, L, N_KV_HEADS, Q_PER_KV, HEAD_DIM)
    return attend(qg, k, v).reshape(B, L, ATTN_WIDTH)


def mixer_merge(p, attn, lp):
    y_attn = attn @ lp['w_attn_out']
    sc = p[..., OFF_SC:OFF_CF]
    b_gate, c_gate, h_sc = jnp.split(sc, 3, axis=-1)
    y_sc = (b_gate * dwconv(c_gate * h_sc, lp['w_sc_conv'])) @ lp['w_sc_out']
    cf_a, cf_g = jnp.split(p[..., OFF_CF:OFF_GATE], 2, axis=-1)
    u = cf_a * jax.nn.sigmoid(cf_g)
    u = dwconv(u, lp['w_cf_conv']) + lp['b_cf_conv']
    u = jax.nn.silu(layernorm(u, lp['g_cf_ln'], lp['b_cf_ln']))
    y_cf = u @ lp['w_cf_out'] + lp['b_cf_out']
    g_a, g_b, g_c = jnp.split(jax.nn.sigmoid(p[..., OFF_GATE:]), N_BRANCHES, axis=-1)
    merged = g_a * y_attn + g_b * y_sc + g_c * y_cf
    return merged @ lp['w_o']


def squared_relu_mlp(h, w1, w2):
    return jnp.square(jax.nn.relu(h @ w1)) @ w2


def setup_inputs(seed: int = 0) -> dict:
    key = jax.random.key(seed)
    ks = jax.random.split(key, 32)
    f32 = jnp.float32
    nrm = lambda k, shape, s: jax.random.normal(k, shape, f32) * s
    L = DEPTH
    return {
        "x": nrm(ks[0], (BATCH, SEQ, D_MODEL), 1.0),
        "c": nrm(ks[1], (BATCH, D_MODEL), 1.0),
        "ctx": nrm(ks[2], (BATCH, CTX_LEN, D_MODEL), 1.0),
        "c_ctx": nrm(ks[3], (D_MODEL,), 1.0),
        "w_mod": nrm(ks[4], (L, D_MODEL, 6 * D_MODEL), 0.5 * D_MODEL ** -0.5),
        "b_mod": nrm(ks[5], (L, 6 * D_MODEL), 0.01),
        "g_norm1": 1.0 + nrm(ks[6], (L, D_MODEL), 0.02),
        "g_norm2": 1.0 + nrm(ks[7], (L, D_MODEL), 0.02),
        "w_in": nrm(ks[8], (L, D_MODEL, D_IN), D_MODEL ** -0.5),
        "q_gain": 1.0 + nrm(ks[9], (L, HEAD_DIM), 0.02),
        "k_gain": 1.0 + nrm(ks[10], (L, HEAD_DIM), 0.02),
        "w_attn_out": nrm(ks[11], (L, ATTN_WIDTH, D_MODEL), ATTN_WIDTH ** -0.5),
        "w_sc_conv": nrm(ks[12], (L, SC_KERNEL, SC_WIDTH), SC_KERNEL ** -0.5),
        "w_sc_out": nrm(ks[13], (L, SC_WIDTH, D_MODEL), SC_WIDTH ** -0.5),
        "w_cf_conv": nrm(ks[14], (L, CF_KERNEL, CF_WIDTH), CF_KERNEL ** -0.5),
        "b_cf_conv": nrm(ks[15], (L, CF_WIDTH), 0.01),
        "g_cf_ln": 1.0 + nrm(ks[16], (L, CF_WIDTH), 0.02),
        "b_cf_ln": nrm(ks[17], (L, CF_WIDTH), 0.01),
        "w_cf_out": nrm(ks[18], (L, CF_WIDTH, D_MODEL), CF_WIDTH ** -0.5),
        "b_cf_out": nrm(ks[19], (L, D_MODEL), 0.01),
        "w_o": nrm(ks[20], (L, D_MODEL, D_MODEL), D_MODEL ** -0.5),
        "w_mlp_in": nrm(ks[21], (L, D_MODEL, D_FF), D_MODEL ** -0.5),
        "w_mlp_out": nrm(ks[22], (L, D_FF, D_MODEL), D_FF ** -0.5),
        "g_final": 1.0 + nrm(ks[23], (D_MODEL,), 0.02),
    }


def reference(x, c, ctx, c_ctx, w_mod, b_mod, g_norm1, g_norm2, w_in, q_gain, k_gain,
              w_attn_out, w_sc_conv, w_sc_out, w_cf_conv, b_cf_conv, g_cf_ln, b_cf_ln,
              w_cf_out, b_cf_out, w_o, w_mlp_in, w_mlp_out, g_final):
    cos, sin = axial_rope_tables(x.shape[1])
    for l in range(DEPTH):
        last = l == DEPTH - 1
        lp = dict(w_attn_out=w_attn_out[l], w_sc_conv=w_sc_conv[l], w_sc_out=w_sc_out[l],
                  w_cf_conv=w_cf_conv[l], b_cf_conv=b_cf_conv[l], g_cf_ln=g_cf_ln[l],
                  b_cf_ln=b_cf_ln[l], w_cf_out=w_cf_out[l], b_cf_out=b_cf_out[l], w_o=w_o[l])
        sh1, sc1, gt1, sh2, sc2, gt2 = modulation(c, w_mod[l], b_mod[l])
        csh1, csc1, cgt1, csh2, csc2, cgt2 = modulation(c_ctx[None, :], w_mod[l], b_mod[l])

        h_lat = rmsnorm(x, g_norm1[l]) * (1.0 + sc1) + sh1
        h_ctx = rmsnorm(ctx, g_norm1[l]) * (1.0 + csc1) + csh1
        p_lat = h_lat @ w_in[l]
        if last:
            p_ctx_kv = h_ctx @ w_in[l][:, OFF_K:OFF_SC]
        else:
            p_ctx = h_ctx @ w_in[l]
            p_ctx_kv = p_ctx[..., OFF_K:OFF_SC]
        k_ctx, v_ctx = keys_values(p_ctx_kv, k_gain[l])
        q_lat = apply_rope(query_heads(p_lat, q_gain[l]), cos, sin)
        k_lat, v_lat = keys_values(p_lat[..., OFF_K:OFF_SC], k_gain[l])
        k_lat = apply_rope(k_lat, cos, sin)
        attn_lat = latent_attention(q_lat, k_lat, v_lat, k_ctx, v_ctx)
        x_new = x + gt1 * mixer_merge(p_lat, attn_lat, lp)

        h2 = rmsnorm(x_new, g_norm2[l]) * (1.0 + sc2) + sh2
        x_new = x_new + gt2 * squared_relu_mlp(h2, w_mlp_in[l], w_mlp_out[l])

        if not last:
            attn_ctx = context_attention(query_heads(p_ctx, q_gain[l]), k_ctx, v_ctx)
            ctx = ctx + cgt1 * mixer_merge(p_ctx, attn_ctx, lp)
            hc2 = rmsnorm(ctx, g_norm2[l]) * (1.0 + csc2) + csh2
            ctx = ctx + cgt2 * squared_relu_mlp(hc2, w_mlp_in[l], w_mlp_out[l])
        x = x_new
    return rmsnorm(x, g_final)
```

```python
import os
import numpy as np
from contextlib import ExitStack
import concourse.bass as bass
import concourse.mybir as mybir
from concourse.bass_utils import run_bass_kernel_spmd

F32 = mybir.dt.float32
BF16 = mybir.dt.bfloat16
AF = mybir.ActivationFunctionType
ALU = mybir.AluOpType

D = 1024
KC = 8
HD = 128
NQH = 8
NKVH = 2
CTXL = 256
DFF = 4096
FC = 32
DIN = 9728
L = 2
TT = 256
NSLOT = 4
NG = 52
NORM_EPS = 1e-6
LN_EPS = 1e-5
EPOCH = 50000
R_G1, R_G2, R_BCFC, R_GLN, R_BLN, R_BCFO, R_BMOD, R_WSC, R_WCF, NPV = 0, 1, 2, 3, 4, 5, 6, 12, 15, 46
G_Q, G_KV, G_SCB, G_SCC, G_SCH, G_CFA, G_CFG, G_GA, G_GB, G_GC = 0, 2, 3, 5, 7, 9, 11, 13, 15, 17
G_AO, G_SO, G_CO, G_WO, G_M1, G_M2, G_DCF, G_DSC = 19, 21, 23, 25, 27, 35, 43, 51


class Op:
    __slots__ = ("eng", "fn", "deps", "sig", "cnt", "dkey", "dval", "is_dma", "waits")


class Sched:
    ENGS = ("pe", "act", "dve", "pool", "sp")

    def __init__(self):
        self.q = {e: [] for e in self.ENGS}
        self.lastw = {}
        self.rd = {}
        self.dmacnt = {}

    def add(self, eng, fn, r=(), w=(), dma=None):
        r = [("ps", k[1] // 2) if (isinstance(k, tuple) and k[0] == "ps") else k for k in r]
        w = [("ps", k[1] // 2) if (isinstance(k, tuple) and k[0] == "ps") else k for k in w]
        op = Op()
        op.eng = eng
        op.fn = fn
        op.is_dma = dma is not None
        op.sig = False
        op.cnt = 0
        op.dkey = dma
        op.dval = 0
        if dma is not None:
            n = self.dmacnt.get(dma, 0) + 1
            self.dmacnt[dma] = n
            op.dval = 16 * n
        deps = {}
        lastw = self.lastw
        rd = self.rd

        def consider(d, raw):
            if d is op:
                return
            if (not d.is_dma) and (not op.is_dma) and d.eng == eng and not raw:
                return
            deps[id(d)] = d

        for k in r:
            d = lastw.get(k)
            if d is not None:
                consider(d, True)
        for k in w:
            d = lastw.get(k)
            if d is not None:
                consider(d, False)
            rr = rd.get(k)
            if rr:
                for e2, o in rr.items():
                    if e2 == "dma":
                        for o2 in o:
                            consider(o2, False)
                    else:
                        consider(o, False)
        for k in r:
            rr = rd.get(k)
            if rr is None:
                rr = rd[k] = {}
            if op.is_dma:
                rr.setdefault("dma", []).append(op)
            else:
                rr[eng] = op
        for k in w:
            lastw[k] = op
            rd[k] = {}
        op.deps = []
        op.waits = {}
        for d in deps.values():
            if d.is_dma:
                key = ("d", d.dkey)
                val = 16 * self.dmacnt[d.dkey]
                if op.dkey == d.dkey:
                    val -= 16
                if op.waits.get(key, 0) < val:
                    op.waits[key] = val
            else:
                d.sig = True
                op.deps.append(d)
        self.q[eng].append(op)
        return op

    def finalize(self):
        self.nsig = {}
        for e in self.ENGS:
            c = 0
            for op in self.q[e]:
                if op.sig and not op.is_dma:
                    c += 1
                    op.cnt = c
            self.nsig[e] = c
        for e in self.ENGS:
            seen = {}
            for op in self.q[e]:
                waits = op.waits
                for d in op.deps:
                    ep = (d.cnt - 1) // EPOCH
                    key = ("e", d.eng, ep)
                    val = d.cnt - ep * EPOCH
                    if waits.get(key, 0) < val:
                        waits[key] = val
                op.waits = []
                for key, val in waits.items():
                    if seen.get(key, 0) >= val:
                        continue
                    seen[key] = val
                    op.waits.append((key, val))
                op.deps = None


def build(S_len, n_layers=L, debug_stage=None):
    nc = bass.Bass("TRN2", target_bir_lowering=False)
    T = TT
    assert T == CTXL and S_len % T == 0
    NT = S_len // T
    NKEY = CTXL + S_len
    NCH = NKEY // 128
    CPT = T // 128

    def din(name, shape, dt=F32):
        return nc.dram_tensor(name, list(shape), dt, kind="ExternalInput").ap()

    def dscr(name, shape, dt):
        return nc.dram_tensor(name, list(shape), dt, kind="Internal").ap()

    x_d = din("x", [S_len, D])
    ctx_d = din("ctx", [CTXL, D])
    cc_d = din("cc", [2, D])
    wmod_d = din("w_mod", [L, D, 6 * D])
    win_d = din("w_in", [L, D, DIN])
    wsq_d = [din(n, [L, D, D]) for n in ("w_attn_out", "w_sc_out", "w_cf_out", "w_o")]
    wm1_d = din("w_mlp_in", [L, D, DFF])
    wm2_d = din("w_mlp_out", [L, DFF, D])
    pv_d = din("pv", [L, NPV, D])
    qk_d = din("qk", [L, 2, HD])
    gfin_d = din("g_final", [D])
    cos_d = din("rope_cos", [128, S_len])
    sin_d = din("rope_sin", [128, S_len])
    ident_d = din("ident", [128, 128])
    perm_d = din("perm", [128, 128])
    out_d = nc.dram_tensor("out", [S_len, D], F32, kind="ExternalOutput").ap()

    xTd = dscr("xTd", [NT, 128, KC * T], F32)
    hTd = dscr("hTd", [NT, 128, KC * T], BF16)
    uTd = dscr("uTd", [KC, 128, S_len + 30], BF16)
    zTd = dscr("zTd", [KC, 128, S_len + 2], BF16)
    cxTd = dscr("cxTd", [2, 128, KC * T], F32)
    chTd = dscr("chTd", [2, 128, KC * T], BF16)
    cuTd = dscr("cuTd", [KC, 128, CTXL + 30], BF16)
    czTd = dscr("czTd", [KC, 128, CTXL + 2], BF16)
    wbf = dscr("wbf", [L, NG, 128, 4096], BF16)

    S = Sched()
    es = ExitStack()

    def sb(name, shape, dt):
        return es.enter_context(nc.sbuf_tensor("sb_" + name, list(shape), dt))

    Kres = sb("Kres", [128, NKVH, NKEY], BF16)
    Vres = sb("Vres", [128, NCH, NKVH * HD], BF16)
    wslot = [sb(f"ws{i}", [128, 4096], BF16) for i in range(NSLOT)]
    xT = sb("xT", [128, KC, T], F32)
    hT = [sb(f"hT{i}", [128, KC, T], BF16) for i in range(2)]
    uwin = sb("uwin", [128, KC, T + 30], BF16)
    zwin = sb("zwin", [128, KC, T + 2], BF16)
    cs = [sb(f"cs{i}", [128, 2, T], F32) for i in range(2)]
    pa = sb("pa", [128, 32, T], BF16)
    big32 = sb("big32", [128, KC, T], F32)
    mh = sb("mh", [128, KC, T], BF16)
    pt = [sb(f"pt{i}", [128, 4, T], BF16) for i in range(3)]
    xtok = sb("xtok", [128, 2, D], F32)
    sq = sb("sq", [128, KC, T], BF16)
    NTMP = 8
    tmp = [sb(f"tmp{i}", [128, T], F32) for i in range(NTMP)]
    tmpb = [sb(f"tmpb{i}", [128, T], BF16) for i in range(4)]
    qnb = [sb(f"qnb{i}", [128, T], BF16) for i in range(4)]
    ps2b = [sb(f"ps2b{i}", [128, 2, T], BF16) for i in range(2)]
    ps1b = [sb(f"ps1b{i}", [128, T], BF16) for i in range(2)]
    ident = sb("ident", [128, 128], F32)
    identb = sb("identb", [128, 128], BF16)
    permf = sb("permf", [128, 128], F32)
    permb = sb("permb", [128, 128], BF16)
    onesb = sb("onesb", [128, 128], BF16)
    zeros = sb("zeros", [128, KC, 16], BF16)
    pvrow = sb("pvrow", [128, D], F32)
    ccrow = sb("ccrow", [128, D], F32)
    qkrow = sb("qkrow", [128, HD], F32)
    gfin = sb("gfin", [128, D], F32)
    scT = sb("scT", [128, KC, 2], F32)
    pvT = [sb(f"pvT{l}", [128, KC, NPV], F32) for l in range(L)]
    qkT = [sb(f"qkT{l}", [128, 2], F32) for l in range(L)]
    modT = [sb(f"modT{l}", [128, 6, KC, 2], F32) for l in range(L)]
    avec = [sb(f"avec{l}", [128, 2, 2, KC], F32) for l in range(L)]
    lnv = [sb(f"lnv{l}", [128, 4, KC], F32) for l in range(L)]
    small = sb("small", [128, 8], F32)
    eps_a = sb("eps_a", [128, 1], F32)
    eps_b = sb("eps_b", [128, 1], F32)
    eps_c = sb("eps_c", [128, 1], F32)
    psum = es.enter_context(nc.psum_tensor("psum", [128, 16, 256], F32))

    if os.environ.get("KDUMP"):
        print("SBUF remaining", nc.sbuf_bytes_remaining)
    tmp_i = [0]

    def T32():
        i = tmp_i[0] % NTMP
        tmp_i[0] += 1
        return tmp[i], ("tmp", i)

    tmpb_i = [0]

    def TB16():
        i = tmpb_i[0] % 4
        tmpb_i[0] += 1
        return tmpb[i], ("tmpb", i)

    ps_rot = {"gen": [12, 14], "all": [0, 2, 4, 6, 8, 10, 12, 14]}
    ps_idx = {"gen": 0, "all": 0}

    cur_pool = ["all"]

    def PS(pool=None):
        pool = pool or cur_pool[0]
        lst = ps_rot[pool]
        i = lst[ps_idx[pool] % len(lst)]
        ps_idx[pool] += 1
        return i

    wcount = [0]

    def wload(src_ap, rkeys, f32=False, ncol=4096):
        s = wcount[0] % NSLOT
        wcount[0] += 1
        dst = wslot[s][:, 0:ncol]
        if f32:
            dst = dst.bitcast(F32)
        S.add("sp", lambda e, dst=dst, src=src_ap: e.dma_start(out=dst, in_=src),
              r=rkeys, w=[("ws", s)], dma=("ws", s))
        return wslot[s], ("ws", s)

    def wg(l, g):
        rk = [("wbf", l, g)] if not (G_M2 <= g < G_M2 + 8) else [("wbf", l, g, q4) for q4 in range(4)]
        ncol = 3072 if g == G_DSC else (3968 if g >= G_DCF else 4096)
        slot, key = wload(wbf[l, g, :, 0:ncol], rk, ncol=ncol)
        return slot[:], key

    def k8(slot_ap, n=512):
        return slot_ap.rearrange("p (k n) -> p k n", k=KC)

    def mmgroup(ps_ap, pairs, r, w, start=True, stop=True):
        n = len(pairs)

        def fn(e):
            ins = None
            for i, (a, b) in enumerate(pairs):
                ins = e.matmul(ps_ap, a, b, start=(start and i == 0), stop=(stop and i == n - 1))
            return ins
        S.add("pe", fn, r=r, w=w)

    def act(out, in_, func, r, w, bias=None, scale=None, accum_out=None):
        kw = {}
        if bias is not None:
            kw["bias"] = bias
        if scale is not None:
            kw["scale"] = scale
        if accum_out is not None:
            kw["accum_out"] = accum_out
        S.add("act", lambda e: e.activation(out=out, in_=in_, func=func, **kw), r=r, w=w)

    def tt(eng, out, in0, in1, op, r, w):
        S.add(eng, lambda e: e.tensor_tensor(out=out, in0=in0, in1=in1, op=op), r=r, w=w)

    def ts(eng, out, in0, s1, s2, op0, op1, r, w):
        if s2 is None:
            S.add(eng, lambda e: e.tensor_scalar(out=out, in0=in0, scalar1=s1, scalar2=None, op0=op0), r=r, w=w)
        else:
            S.add(eng, lambda e: e.tensor_scalar(out=out, in0=in0, scalar1=s1, scalar2=s2, op0=op0, op1=op1), r=r, w=w)

    def stt(eng, out, in0, scalar, in1, op0, op1, r, w):
        S.add(eng, lambda e: e.scalar_tensor_tensor(out=out, in0=in0, scalar=scalar, in1=in1, op0=op0, op1=op1), r=r, w=w)

    def cp(eng, out, in_, r, w):
        if eng == "act":
            S.add("act", lambda e: e.activation(out=out, in_=in_, func=AF.Copy), r=r, w=w)
        else:
            S.add(eng, lambda e: e.tensor_copy(out=out, in_=in_), r=r, w=w)

    def dma(q, out, in_, r, w, key, slow=False):
        if slow:
            S.add(q, lambda e: e.dma_start(out=out, in_=in_, allow_slow_non_contiguous=True), r=r, w=w, dma=key)
        else:
            S.add(q, lambda e: e.dma_start(out=out, in_=in_), r=r, w=w, dma=key)

    def rsqrt_from(ps_or_sb, rkeys, scale, eps):
        t1, k1 = T32()
        act(t1[:], ps_or_sb, AF.Sqrt, r=rkeys, w=[k1], scale=scale, bias=eps_ap(eps))
        t2, k2 = T32()
        S.add("dve", lambda e, t1=t1, t2=t2: e.reciprocal(out=t2[:], in_=t1[:]), r=[k1], w=[k2])
        return t2, k2

    eps_tiles = {}

    def eps_ap(v):
        return eps_tiles[v]

    S.add("dve", lambda e: e.memset(onesb[:], 1.0), w=["onesb"])
    S.add("dve", lambda e: e.memset(zeros[:], 0.0), w=["zeros"])
    S.add("dve", lambda e: e.memset(eps_a[:], NORM_EPS), w=["small"])
    S.add("dve", lambda e: e.memset(eps_b[:], LN_EPS), w=["small"])
    S.add("dve", lambda e: e.memset(eps_c[:], 128.0 * NORM_EPS), w=["small"])
    eps_tiles[NORM_EPS] = eps_a[:]
    eps_tiles[LN_EPS] = eps_b[:]
    dma("act", ident[:], ident_d, [], ["ident"], "c_ident")
    dma("act", permf[:], perm_d, [], ["permf"], "c_perm")
    dma("act", ccrow[0:2, :], cc_d, [], ["ccrow"], "c_cc")
    dma("act", gfin[:], gfin_d.partition_broadcast(128), [], ["gfin"], "c_gfin")
    cp("dve", identb[:], ident[:], ["ident"], ["identb"])
    cp("dve", permb[:], permf[:], ["permf"], ["permb"])
    for (td, n, padw, nm) in ((uTd, S_len, 15, "luTd"), (zTd, S_len, 1, "lzTd"), (cuTd, CTXL, 15, "cuTd"), (czTd, CTXL, 1, "czTd")):
        dma("pool", td[:, :, 0:padw].rearrange("k p t -> p k t"), zeros[:, :, 0:padw], ["zeros"], [(nm, "padl")], "c_pad", slow=True)
        dma("pool", td[:, :, padw + n:padw + n + padw].rearrange("k p t -> p k t"), zeros[:, :, 0:padw], ["zeros"], [(nm, "padr")], "c_pad", slow=True)

    def cast_group(l, g):
        dst = wbf[l, g]
        if g < 19:
            src = win_d[l, :, g * 512:(g + 1) * 512].rearrange("(k p) n -> p k n", p=128)
            dv = dst.rearrange("p (k n) -> p k n", k=KC)
        elif g < 27:
            wi, h = (g - 19) // 2, (g - 19) % 2
            src = wsq_d[wi][l, :, h * 512:(h + 1) * 512].rearrange("(k p) n -> p k n", p=128)
            dv = dst.rearrange("p (k n) -> p k n", k=KC)
        elif g < 35:
            src = wm1_d[l, :, (g - 27) * 512:(g - 26) * 512].rearrange("(k p) n -> p k n", p=128)
            dv = dst.rearrange("p (k n) -> p k n", k=KC)
        else:
            dc = g - 35
            src = wm2_d[l, :, dc * 128:(dc + 1) * 128].rearrange("(k p) n -> p k n", p=128)
            dv = dst.rearrange("p (k n) -> p k n", k=FC)
            for q4 in range(4):
                dma("pool", dv[:, q4 * 8:(q4 + 1) * 8, :], src[:, q4 * 8:(q4 + 1) * 8, :], [], [("wbf", l, g, q4)], ("cast", l, 2))
            return
        dma("pool", dv, src, [], [("wbf", l, g)], ("cast", l, 0 if g in (2, 5, 6, 7, 8, 9, 10, 11, 12) else 1))

    cast_order = [2, 5, 6, 7, 8, 9, 10, 11, 12, 0, 1, 3, 4] + list(range(13, 43))
    for l in range(n_layers):
        for g in cast_order:
            cast_group(l, g)

    act(ccrow[0:2, :], ccrow[0:2, :], AF.Silu, r=["ccrow"], w=["ccrow"])
    S.add("pe", lambda e: [e.transpose(psum[:, 0, 2 * k:2 * k + 2], ccrow[0:2, k * 128:(k + 1) * 128], ident[0:2, 0:2]) for k in range(KC)][-1],
          r=["ccrow", "ident"], w=[("ps", 0)])
    cp("dve", scT[:].rearrange("p k c -> p (k c)"), psum[:, 0, 0:16], [("ps", 0)], ["scT"])

    def prep(l):
        dma("act", pvrow[0:NPV, :], pv_d[l], [], ["pvrow"], "c_pv")
        dma("act", qkrow[0:2, :], qk_d[l], [], ["qkrow"], "c_qk")
        bank = psum[:, 0:2, :].rearrange("p a b -> p (a b)")
        S.add("pe", lambda e: [e.transpose(bank[:, k * NPV:(k + 1) * NPV], pvrow[0:NPV, k * 128:(k + 1) * 128], ident[0:NPV, 0:NPV]) for k in range(KC)][-1],
              r=["pvrow", "ident"], w=[("ps", 0), ("ps", 1)])
        cp("dve", pvT[l][:].rearrange("p k r -> p (k r)"), bank[:, 0:KC * NPV], [("ps", 0), ("ps", 1)], [("pvT", l)])
        S.add("pe", lambda e: e.transpose(psum[:, 2, 0:2], qkrow[0:2, :], ident[0:2, 0:2]), r=["qkrow", "ident"], w=[("ps", 2)])
        ts("dve", qkT[l][:], psum[:, 2, 0:2], float(np.sqrt(128.0)), None, ALU.mult, None, [("ps", 2)], [("qkT", l)])
        if debug_stage == 1.1:
            return
        mps = 3
        for j2 in range(24):
            src = wmod_d[l, :, j2 * 256:(j2 + 1) * 256].rearrange("(k p) n -> p k n", p=128)
            s = wcount[0] % NSLOT
            wcount[0] += 1
            dstv = wslot[s][:].bitcast(F32).rearrange("p (k n) -> p k n", k=KC)
            S.add("sp", lambda e, dstv=dstv, src=src: e.dma_start(out=dstv, in_=src), r=[], w=[("ws", s)], dma=("ws", s))
            for jj in range(2):
                j = j2 * 2 + jj
                mmgroup(psum[:, mps, 2 * j:2 * j + 2],
                        [(dstv[:, k, jj * 128:(jj + 1) * 128], scT[:, k, :]) for k in range(KC)],
                        r=[("ws", s), "scT"], w=[("ps", mps)])
        if debug_stage == 1.2:
            return
        for sel in range(6):
            tt("dve", modT[l][:, sel, :, :], psum[:, mps, sel * 16:(sel + 1) * 16].rearrange("p (k c) -> p k c", c=2),
               pvT[l][:, :, R_BMOD + sel:R_BMOD + sel + 1].to_broadcast([128, KC, 2]), ALU.add,
               [("ps", mps), ("pvT", l)], [("modT", l)])
        for col in range(2):
            stt("dve", avec[l][:, 0, col, :], modT[l][:, 1, :, col], 1.0, pvT[l][:, :, R_G1], ALU.add, ALU.mult,
                [("modT", l), ("pvT", l)], [("avec", l)])
            stt("dve", avec[l][:, 1, col, :], modT[l][:, 4, :, col], 1.0, pvT[l][:, :, R_G2], ALU.add, ALU.mult,
                [("modT", l), ("pvT", l)], [("avec", l)])
        for i, rr in enumerate((R_GLN, R_BLN, R_BCFC, R_BCFO)):
            cp("dve", lnv[l][:, i, :], pvT[l][:, :, rr], [("pvT", l)], [("lnv", l)])
        if debug_stage == 1.3:
            return
        stg = pa[:, 0:16, :].rearrange("p a b -> p (a b)")
        stgk = [("pa", i) for i in range(16)]
        for j in range(KC):
            for k in range(31):
                ts("dve", stg[:, k * 128:(k + 1) * 128], identb[:], pvT[l][:, j, R_WCF + k:R_WCF + k + 1], None, ALU.mult, None,
                   ["identb", ("pvT", l)], stgk)
            dma("pool", wbf[l, G_DCF + j, :, 0:31 * 128], stg[:, 0:31 * 128], stgk, [("wbf", l, G_DCF + j)], ("dg", l))
        for j in range(KC):
            for k in range(3):
                ts("dve", stg[:, (j * 3 + k) * 128:(j * 3 + k + 1) * 128], identb[:], pvT[l][:, j, R_WSC + k:R_WSC + k + 1], None, ALU.mult, None,
                   ["identb", ("pvT", l)], stgk)
        dma("pool", wbf[l, G_DSC, :, 0:3072], stg[:, 0:3072], stgk, [("wbf", l, G_DSC)], ("dg", l))

    if debug_stage is None or debug_stage >= 1:
        for l in range(n_layers):
            prep(l)

    sqk = [("sq", k) for k in range(KC)]
    b32k = [("big32", k) for k in range(KC)]
    xtk = [("xtok", 0), ("xtok", 1)]

    def headnorm(l, psi, gain_ap, rope, cs_t, cs_key, dst_ap, dst_keys):
        sqb, sqk = TB16()
        act(sqb[:], psum[:, psi, :], AF.Square, r=[("ps", psi)], w=[sqk])
        HN = int(os.environ.get("HN", "9"))
        if HN < 1:
            return
        p2 = PS()
        mmgroup(psum[:, p2, :], [(onesb[:], sqb[:])], r=["onesb", sqk], w=[("ps", p2)])
        if HN < 2:
            return
        t1, k1 = T32()
        act(t1[:], psum[:, p2, :], AF.Sqrt, r=[("ps", p2), "small"], w=[k1], bias=eps_c[:], scale=1.0)
        rstd, kr = T32()
        S.add("dve", lambda e, rstd=rstd, t1=t1: e.reciprocal(out=rstd[:], in_=t1[:]), r=[k1], w=[kr])
        tg, ktg = T32()
        ts("dve", tg[:], psum[:, psi, :], gain_ap, None, ALU.mult, None, [("ps", psi), ("qkT", l)], [ktg])
        if not rope:
            tt("dve", dst_ap, tg[:], rstd[:], ALU.mult, [ktg, kr], dst_keys)
            return
        qn, kq = TB16()
        tt("dve", qn[:], tg[:], rstd[:], ALU.mult, [ktg, kr], [kq])
        p3 = PS()
        mmgroup(psum[:, p3, :], [(permb[:], qn[:])], r=["permb", kq], w=[("ps", p3)])
        ta, ka = T32()
        tt("pool", ta[:], qn[:], cs_t[:, 0, :], ALU.mult, [kq, cs_key], [ka])
        tb, kb = T32()
        tt("dve", tb[:], psum[:, p3, :], cs_t[:, 1, :], ALU.mult, [("ps", p3), cs_key], [kb])
        tt("dve", dst_ap, ta[:], tb[:], ALU.add, [ka, kb], dst_keys)

    def phase1(l, kind, ti):
        lat = kind == "lat"
        col = 0 if lat else 1
        t0 = ti * T
        kch0 = (CTXL // 128 + ti * CPT) if lat else 0
        keyoff = CTXL + t0 if lat else 0
        xsrc_tok = x_d if lat else ctx_d
        xtd = xTd if lat else cxTd
        htd = hTd if lat else chTd
        utd = uTd if lat else cuTd
        ztd = zTd if lat else czTd
        nm = "l" if lat else "c"
        xkey = "xT"
        hbuf = hT[ti % 2]
        hkey = ("hT", ti % 2)
        if l == 0:
            for b in range(2):
                dma("act", xtok[:, b, :], xsrc_tok[t0 + b * 128:t0 + (b + 1) * 128, :], [], [("xtok", b)], "xtok")
            for k in range(KC):
                pi = PS("all")
                S.add("pe", lambda e, pi=pi, k=k: [e.transpose(psum[:, pi, b * 128:(b + 1) * 128], xtok[:, b, k * 128:(k + 1) * 128], ident[:]) for b in range(2)][-1],
                      r=xtk + ["ident"], w=[("ps", pi)])
                cp("act" if k % 2 else "dve", xT[:, k, :], psum[:, pi, :], [("ps", pi)], [xkey])
            if debug_stage != 2.05:
                _v = os.environ.get("XV", "")
                _src = big32 if _v == "big32" else xT
                _dst = xTd[0] if _v == "xTd" else xtd[ti]
                dma("sp", _dst, _src[:].rearrange("p k t -> p (k t)"), [xkey], [(nm + "xTd", ti)], "xst")
        else:
            dma("act", xT[:].rearrange("p k t -> p (k t)"), xtd[ti], [(nm + "xTd", ti)], [xkey], "xld")
        if lat:
            cst = cs[ti % 2]
            cskey = ("cs", ti % 2)
            dma("act", cst[:, 0, :], cos_d[:, t0:t0 + T], [], [cskey], cskey)
            dma("act", cst[:, 1, :], sin_d[:, t0:t0 + T], [], [cskey], cskey)
        else:
            cst, cskey = None, None
        if debug_stage == 2.05 and os.environ.get("XV") == "dummy":
            dma("act", xtok[:], xsrc_tok[t0:t0 + T, :].rearrange("(b p) d -> p b d", p=128), [], xtk, "xtok")
        if debug_stage in (2.1, 2.05):
            return
        act(sq[:].rearrange("p k t -> p (k t)"), xT[:].rearrange("p k t -> p (k t)"), AF.Square, r=[xkey], w=sqk)
        pss = PS("all")
        mmgroup(psum[:, pss, :], [(onesb[:], sq[:, k, :]) for k in range(KC)], r=["onesb"] + sqk, w=[("ps", pss)])
        rstd, kr = rsqrt_from(psum[:, pss, :], [("ps", pss), "small"], 1.0 / D, NORM_EPS)
        tt("dve", big32[:], xT[:], rstd[:].unsqueeze(1).to_broadcast([128, KC, T]), ALU.mult, [xkey, kr], b32k)
        for k in range(KC):
            ts("dve", hbuf[:, k, :], big32[:, k, :], avec[l][:, 0, col, k:k + 1], modT[l][:, 0, k, col:col + 1],
               ALU.mult, ALU.add, [("big32", k), ("avec", l), ("modT", l)], [hkey])
        dma("sp", htd[ti], hbuf[:].rearrange("p k t -> p (k t)"), [hkey], [(nm + "hTd", ti)], "hst")
        if debug_stage == 2.2:
            return
        wkv, kkv = wg(l, G_KV)
        wkv8 = k8(wkv)
        for j in range(NKVH):
            pi = PS("all")
            mmgroup(psum[:, pi, :], [(wkv8[:, k, j * 128:(j + 1) * 128], hbuf[:, k, :]) for k in range(KC)],
                    r=[kkv, hkey], w=[("ps", pi)])
            if debug_stage == 2.21:
                continue
            headnorm(l, pi, qkT[l][:, 1:2], lat, cst, cskey, Kres[:, j, keyoff:keyoff + T],
                     [("K", kch0 + c) for c in range(CPT)])
        if debug_stage in (2.21, 2.22):
            return
        for b in range(CPT):
            pi = PS("all")
            mmgroup(psum[:, pi, :], [(hbuf[:, k, b * 128:(b + 1) * 128], wkv8[:, k, 256:512]) for k in range(KC)],
                    r=[kkv, hkey], w=[("ps", pi)])
            cp("act", Vres[:, kch0 + b, :], psum[:, pi, :], [("ps", pi)], [("V", kch0 + b)])
        if debug_stage == 2.3:
            return
        for half in range(2):
            wa, ka_ = wg(l, G_CFA + half)
            wgt, kg_ = wg(l, G_CFG + half)
            wa8, wg8 = k8(wa), k8(wgt)
            for oc in range(4):
                j = half * 4 + oc
                pa_i = PS("all")
                pg_i = PS("all")
                mmgroup(psum[:, pa_i, :], [(wa8[:, k, oc * 128:(oc + 1) * 128], hbuf[:, k, :]) for k in range(KC)], r=[ka_, hkey], w=[("ps", pa_i)])
                mmgroup(psum[:, pg_i, :], [(wg8[:, k, oc * 128:(oc + 1) * 128], hbuf[:, k, :]) for k in range(KC)], r=[kg_, hkey], w=[("ps", pg_i)])
                sg, ksg = T32()
                act(sg[:], psum[:, pg_i, :], AF.Sigmoid, r=[("ps", pg_i)], w=[ksg])
                tt("dve", pa[:, 24 + j, :], psum[:, pa_i, :], sg[:], ALU.mult, [("ps", pa_i), ksg], [("pa", 24 + j)])
        dma("sp", utd[:, :, 15 + t0:15 + t0 + T].rearrange("k p t -> p k t"), pa[:, 24:32, :],
            [("pa", 24 + j) for j in range(KC)], [(nm + "uTd", ti)], "ust")
        if debug_stage == 2.4:
            return
        for half in range(2):
            wc, kc_ = wg(l, G_SCC + half)
            wh, kh_ = wg(l, G_SCH + half)
            wc8, wh8 = k8(wc), k8(wh)
            for oc in range(4):
                j = half * 4 + oc
                pc_i = PS("all")
                ph_i = PS("all")
                mmgroup(psum[:, pc_i, :], [(wc8[:, k, oc * 128:(oc + 1) * 128], hbuf[:, k, :]) for k in range(KC)], r=[kc_, hkey], w=[("ps", pc_i)])
                mmgroup(psum[:, ph_i, :], [(wh8[:, k, oc * 128:(oc + 1) * 128], hbuf[:, k, :]) for k in range(KC)], r=[kh_, hkey], w=[("ps", ph_i)])
                cg, kcg = T32()
                cp("act", cg[:], psum[:, pc_i, :], [("ps", pc_i)], [kcg])
                tt("dve", pa[:, 16 + j, :], psum[:, ph_i, :], cg[:], ALU.mult, [("ps", ph_i), kcg], [("pa", 16 + j)])
        dma("sp", ztd[:, :, 1 + t0:1 + t0 + T].rearrange("k p t -> p k t"), pa[:, 16:24, :],
            [("pa", 16 + j) for j in range(KC)], [(nm + "zTd", ti)], "zst")

    def p2_loads(l, kind, ti, hi):
        lat = kind == "lat"
        t0 = ti * T
        ntile = NT if lat else 1
        htd = hTd if lat else chTd
        utd = uTd if lat else cuTd
        ztd = zTd if lat else czTd
        nm = "l" if lat else "c"
        hbuf = hT[hi % 2]
        hkey = ("hT", hi % 2)
        dma("act", hbuf[:].rearrange("p k t -> p (k t)"), htd[ti], [(nm + "hTd", ti)], [hkey], hkey)
        if lat:
            cst = cs[hi % 2]
            cskey = ("cs", hi % 2)
            dma("act", cst[:, 0, :], cos_d[:, t0:t0 + T], [], [cskey], cskey)
            dma("act", cst[:, 1, :], sin_d[:, t0:t0 + T], [], [cskey], cskey)
        nbr = [(nm + "uTd", i) for i in (ti - 1, ti, ti + 1) if 0 <= i < ntile] + [(nm + "uTd", "padl"), (nm + "uTd", "padr")]
        dma("act", uwin[:], utd[:, :, t0:t0 + T + 30].rearrange("k p t -> p k t"), nbr, ["uwin"], "uwin")
        nbr = [(nm + "zTd", i) for i in (ti - 1, ti, ti + 1) if 0 <= i < ntile] + [(nm + "zTd", "padl"), (nm + "zTd", "padr")]
        dma("act", zwin[:], ztd[:, :, t0:t0 + T + 2].rearrange("k p t -> p k t"), nbr, ["zwin"], "zwin")

    def p2_xload(kind, ti):
        lat = kind == "lat"
        xtd = xTd if lat else cxTd
        nm = "l" if lat else "c"
        dma("act", xT[:].rearrange("p k t -> p (k t)"), xtd[ti], [(nm + "xTd", ti)], ["xT"], "xld")

    def phase2(l, kind, ti, hi, last, nxt):
        lat = kind == "lat"
        col = 0 if lat else 1
        t0 = ti * T
        xtd = xTd if lat else cxTd
        nm = "l" if lat else "c"
        xkey = "xT"
        hbuf = hT[hi % 2]
        hkey = ("hT", hi % 2)
        if lat:
            cst = cs[hi % 2]
            cskey = ("cs", hi % 2)
        else:
            cst, cskey = None, None
        cur_pool[0] = "gen"
        wqs = {}
        gain_ap = qkT[l][:, 0:1]
        st = {}

        def qs0(h):
            if h % 4 == 0:
                wqs[h // 4] = wg(l, G_Q + h // 4)
            wq, kq_ = wqs[h // 4]
            oc = h % 4
            pi = (h % 4) * 2
            mmgroup(psum[:, pi, :], [(k8(wq)[:, k, oc * 128:(oc + 1) * 128], hbuf[:, k, :]) for k in range(KC)], r=[kq_, hkey], w=[("ps", pi)])
            st[h] = {"pi": pi}

        def qs1(h):
            pi = st[h]["pi"]
            sqb, sqk_ = TB16()
            act(sqb[:], psum[:, pi, :], AF.Square, r=[("ps", pi)], w=[sqk_])
            p2 = 8 + (h % 2) * 2
            mmgroup(psum[:, p2, :], [(onesb[:], sqb[:])], r=["onesb", sqk_], w=[("ps", p2)])
            st[h]["p2"] = p2

        def qs2(h):
            pi, p2 = st[h]["pi"], st[h]["p2"]
            t1, k1 = T32()
            act(t1[:], psum[:, p2, :], AF.Sqrt, r=[("ps", p2), "small"], w=[k1], bias=eps_c[:], scale=1.0)
            rstd, kr = T32()
            S.add("dve", lambda e, rstd=rstd, t1=t1: e.reciprocal(out=rstd[:], in_=t1[:]), r=[k1], w=[kr])
            tg, ktg = T32()
            ts("dve", tg[:], psum[:, pi, :], gain_ap, None, ALU.mult, None, [("ps", pi), ("qkT", l)], [ktg])
            if not lat:
                tt("dve", pa[:, h, :], tg[:], rstd[:], ALU.mult, [ktg, kr], [("pa", h)])
                return
            qn = qnb[h % 4]
            tt("dve", qn[:], tg[:], rstd[:], ALU.mult, [ktg, kr], [("qnb", h % 4)])

        def qs3(h):
            if not lat:
                return
            qn = qnb[h % 4]
            kq = ("qnb", h % 4)
            p3 = 12 + (h % 2) * 2
            mmgroup(psum[:, p3, :], [(permb[:], qn[:])], r=["permb", kq], w=[("ps", p3)])
            ta, ka = T32()
            tt("pool", ta[:], qn[:], cst[:, 0, :], ALU.mult, [kq, cskey], [ka])
            tb, kb = T32()
            tt("dve", tb[:], psum[:, p3, :], cst[:, 1, :], ALU.mult, [("ps", p3), cskey], [kb])
            tt("dve", pa[:, h, :], ta[:], tb[:], ALU.add, [ka, kb], [("pa", h)])

        if os.environ.get("SKEW", "1") == "1":
            for step in range(NQH + 3):
                if step < NQH:
                    qs0(step)
                if 0 <= step - 1 < NQH:
                    qs1(step - 1)
                if 0 <= step - 2 < NQH:
                    qs2(step - 2)
                if 0 <= step - 3 < NQH:
                    qs3(step - 3)
        else:
            for h in range(NQH):
                qs0(h)
                qs1(h)
                qs2(h)
                qs3(h)
        chunks = list(range(NCH)) if lat else list(range(CTXL // 128))
        groups = [chunks[i:i + 4] for i in range(0, len(chunks), 4)]
        ngr = len(groups)
        for j in range(NQH):
            kv = j // (NQH // NKVH)
            po, psm = 8, 10
            qap = pa[:, j, :]

            def emit_S(gi, j=j, kv=kv, qap=qap):
                grp = groups[gi]
                base = (gi % 2) * 4
                for ci, c in enumerate(grp):
                    mmgroup(psum[:, base + ci, :], [(Kres[:, kv, c * 128:(c + 1) * 128], qap)],
                            r=[("K", c), ("pa", j)], w=[("ps", base + ci)])
                n = len(grp)
                act(pt[gi % 3][:, 0:n, :], psum[:, base:base + n, :], AF.Exp,
                    r=[("ps", base + ci) for ci in range(n)], w=[("pt", gi % 3)], scale=float(HD ** -0.5))

            def emit_PV(gi, j=j, kv=kv, po=po, psm=psm):
                grp = groups[gi]
                n = len(grp)
                ptb = pt[gi % 3]
                s1 = ps1b[gi % 2]
                OSUM = os.environ.get("OSUM", "1") == "1"
                if not OSUM:
                    pass
                elif n == 4:
                    s2 = ps2b[gi % 2]
                    tt("dve", s2[:], ptb[:, 0:2, :], ptb[:, 2:4, :], ALU.add, [("pt", gi % 3)], [("ps2b", gi % 2)])
                    tt("dve", s1[:], s2[:, 0, :], s2[:, 1, :], ALU.add, [("ps2b", gi % 2)], [("ps1b", gi % 2)])
                else:
                    assert n == 2
                    tt("dve", s1[:], ptb[:, 0, :], ptb[:, 1, :], ALU.add, [("pt", gi % 3)], [("ps1b", gi % 2)])
                for ci, c in enumerate(grp):
                    first = gi == 0 and ci == 0
                    lastc = gi == ngr - 1 and ci == len(grp) - 1
                    mmgroup(psum[:, po, :], [(Vres[:, c, kv * 128:(kv + 1) * 128], ptb[:, ci, :])],
                            r=[("V", c), ("pt", gi % 3)], w=[("ps", po)], start=first, stop=lastc)
                if OSUM:
                    mmgroup(psum[:, psm, :], [(onesb[:], s1[:])],
                            r=["onesb", ("ps1b", gi % 2)], w=[("ps", psm)], start=(gi == 0), stop=(gi == ngr - 1))
                else:
                    for ci, c in enumerate(grp):
                        mmgroup(psum[:, psm, :], [(onesb[:], ptb[:, ci, :])], r=["onesb", ("pt", gi % 3)], w=[("ps", psm)],
                                start=(gi == 0 and ci == 0), stop=(gi == ngr - 1 and ci == n - 1))
            emit_S(0)
            for gi in range(ngr):
                if gi + 1 < ngr:
                    emit_S(gi + 1)
                emit_PV(gi)
            rinv, kri = T32()
            S.add("dve", lambda e, rinv=rinv, psm=psm: e.reciprocal(out=rinv[:], in_=psum[:, psm, :]), r=[("ps", psm)], w=[kri])
            tt("dve", pa[:, 8 + j, :], psum[:, po, :], rinv[:], ALU.mult, [("ps", po), kri], [("pa", 8 + j)])
        cur_pool[0] = "all"
        wdsc, kdsc = wg(l, G_DSC)
        for half in range(2):
            wb_, kb_ = wg(l, G_SCB + half)
            wb8 = k8(wb_)
            for oc in range(4):
                j = half * 4 + oc
                pb = PS()
                mmgroup(psum[:, pb, :], [(wb8[:, k, oc * 128:(oc + 1) * 128], hbuf[:, k, :]) for k in range(KC)], r=[kb_, hkey], w=[("ps", pb)])
                bg, kbg = T32()
                cp("act", bg[:], psum[:, pb, :], [("ps", pb)], [kbg])
                pc = PS()
                mmgroup(psum[:, pc, :], [(wdsc[:, (j * 3 + k) * 128:(j * 3 + k + 1) * 128], zwin[:, j, k:k + T]) for k in range(3)],
                        r=[kdsc, "zwin"], w=[("ps", pc)])
                tt("dve", pa[:, 16 + j, :], psum[:, pc, :], bg[:], ALU.mult, [("ps", pc), kbg], [("pa", 16 + j)])
        for j in range(KC):
            wd, kd = wg(l, G_DCF + j)
            pc = PS()
            mmgroup(psum[:, pc, :], [(wd[:, k * 128:(k + 1) * 128], uwin[:, j, k:k + T]) for k in range(31)],
                    r=[kd, "uwin"], w=[("ps", pc)])
            act(big32[:, j, :], psum[:, pc, :], AF.Identity, r=[("ps", pc), ("lnv", l)], w=[("big32", j)], bias=lnv[l][:, 2, j:j + 1])
            cp("pool", mh[:, j, :], big32[:, j, :], [("big32", j)], [("mh", j)])
            act(sq[:, j, :], big32[:, j, :], AF.Square, r=[("big32", j)], w=[("sq", j)])
        pm = PS()
        pq = PS()
        mmgroup(psum[:, pm, :], [(onesb[:], mh[:, k, :]) for k in range(KC)], r=["onesb"] + [("mh", k) for k in range(KC)], w=[("ps", pm)])
        mmgroup(psum[:, pq, :], [(onesb[:], sq[:, k, :]) for k in range(KC)], r=["onesb"] + [("sq", k) for k in range(KC)], w=[("ps", pq)])
        mean, kmean = T32()
        ts("dve", mean[:], psum[:, pm, :], 1.0 / D, None, ALU.mult, None, [("ps", pm)], [kmean])
        msq, kmsq = T32()
        tt("dve", msq[:], mean[:], mean[:], ALU.mult, [kmean], [kmsq])
        var, kvar = T32()
        stt("dve", var[:], psum[:, pq, :], 1.0 / D, msq[:], ALU.mult, ALU.subtract, [("ps", pq), kmsq], [kvar])
        rstd, kr = rsqrt_from(var[:], [kvar, "small"], 1.0, LN_EPS)
        tt("dve", big32[:], big32[:], mean[:].unsqueeze(1).to_broadcast([128, KC, T]), ALU.subtract, b32k + [kmean], b32k)
        tt("dve", big32[:], big32[:], rstd[:].unsqueeze(1).to_broadcast([128, KC, T]), ALU.mult, b32k + [kr], b32k)
        for j in range(KC):
            act(pa[:, 24 + j, :], big32[:, j, :], AF.Silu, r=[("big32", j), ("lnv", l)], w=[("pa", 24 + j)],
                scale=lnv[l][:, 0, j:j + 1], bias=lnv[l][:, 1, j:j + 1])
        if nxt is not None:
            p2_loads(l, nxt[0], nxt[1], hi + 1)
        for half in range(2):
            for oc in range(4):
                dc = half * 4 + oc
                if oc == 0:
                    wao, kao = wg(l, G_AO + half)
                    wga, kga = wg(l, G_GA + half)
                pya = PS()
                pga = PS()
                mmgroup(psum[:, pya, :], [(k8(wao)[:, k, oc * 128:(oc + 1) * 128], pa[:, 8 + k, :]) for k in range(KC)],
                        r=[kao] + [("pa", 8 + k) for k in range(KC)], w=[("ps", pya)])
                mmgroup(psum[:, pga, :], [(k8(wga)[:, k, oc * 128:(oc + 1) * 128], hbuf[:, k, :]) for k in range(KC)],
                        r=[kga, hkey], w=[("ps", pga)])
                ga, kga_t = T32()
                act(ga[:], psum[:, pga, :], AF.Sigmoid, r=[("ps", pga)], w=[kga_t])
                tt("dve", big32[:, dc, :], psum[:, pya, :], ga[:], ALU.mult, [("ps", pya), kga_t], [("big32", dc)])
        for half in range(2):
            for oc in range(4):
                dc = half * 4 + oc
                if oc == 0:
                    wso, kso = wg(l, G_SO + half)
                    wgb, kgb = wg(l, G_GB + half)
                pys = PS()
                pgb = PS()
                mmgroup(psum[:, pys, :], [(k8(wso)[:, k, oc * 128:(oc + 1) * 128], pa[:, 16 + k, :]) for k in range(KC)],
                        r=[kso] + [("pa", 16 + k) for k in range(KC)], w=[("ps", pys)])
                mmgroup(psum[:, pgb, :], [(k8(wgb)[:, k, oc * 128:(oc + 1) * 128], hbuf[:, k, :]) for k in range(KC)],
                        r=[kgb, hkey], w=[("ps", pgb)])
                gb, kgb_t = T32()
                act(gb[:], psum[:, pgb, :], AF.Sigmoid, r=[("ps", pgb)], w=[kgb_t])
                mb, kmb = T32()
                tt("dve", mb[:], psum[:, pys, :], gb[:], ALU.mult, [("ps", pys), kgb_t], [kmb])
                tt("pool", big32[:, dc, :], big32[:, dc, :], mb[:], ALU.add, [("big32", dc), kmb], [("big32", dc)])
        for half in range(2):
            for oc in range(4):
                dc = half * 4 + oc
                if oc == 0:
                    wco, kco = wg(l, G_CO + half)
                    wgc, kgc = wg(l, G_GC + half)
                pyc = PS()
                pgc = PS()
                mmgroup(psum[:, pyc, :], [(k8(wco)[:, k, oc * 128:(oc + 1) * 128], pa[:, 24 + k, :]) for k in range(KC)],
                        r=[kco] + [("pa", 24 + k) for k in range(KC)], w=[("ps", pyc)])
                mmgroup(psum[:, pgc, :], [(k8(wgc)[:, k, oc * 128:(oc + 1) * 128], hbuf[:, k, :]) for k in range(KC)],
                        r=[kgc, hkey], w=[("ps", pgc)])
                gc, kgc_t = T32()
                act(gc[:], psum[:, pgc, :], AF.Sigmoid, r=[("ps", pgc)], w=[kgc_t])
                mc, kmc = T32()
                stt("dve", mc[:], psum[:, pyc, :], lnv[l][:, 3, dc:dc + 1], gc[:], ALU.add, ALU.mult,
                    [("ps", pyc), kgc_t, ("lnv", l)], [kmc])
                tt("dve", mh[:, dc, :], big32[:, dc, :], mc[:], ALU.add, [("big32", dc), kmc], [("mh", dc)])
        for half in range(2):
            wo, kwo = wg(l, G_WO + half)
            for oc in range(4):
                dc = half * 4 + oc
                po_ = PS()
                mmgroup(psum[:, po_, :], [(k8(wo)[:, k, oc * 128:(oc + 1) * 128], mh[:, k, :]) for k in range(KC)],
                        r=[kwo] + [("mh", k) for k in range(KC)], w=[("ps", po_)])
                stt("dve", xT[:, dc, :], psum[:, po_, :], modT[l][:, 2, dc, col:col + 1], xT[:, dc, :], ALU.mult, ALU.add,
                    [("ps", po_), ("modT", l), xkey], [xkey])
        act(sq[:].rearrange("p k t -> p (k t)"), xT[:].rearrange("p k t -> p (k t)"), AF.Square, r=[xkey], w=sqk)
        pss = PS()
        mmgroup(psum[:, pss, :], [(onesb[:], sq[:, k, :]) for k in range(KC)], r=["onesb"] + [("sq", k) for k in range(KC)], w=[("ps", pss)])
        rstd2, kr2 = rsqrt_from(psum[:, pss, :], [("ps", pss), "small"], 1.0 / D, NORM_EPS)
        tt("dve", big32[:], xT[:], rstd2[:].unsqueeze(1).to_broadcast([128, KC, T]), ALU.mult, [xkey, kr2], b32k)
        for k in range(KC):
            ts("dve", mh[:, k, :], big32[:, k, :], avec[l][:, 1, col, k:k + 1], modT[l][:, 3, k, col:col + 1],
               ALU.mult, ALU.add, [("big32", k), ("avec", l), ("modT", l)], [("mh", k)])
        mhk = [("mh", k) for k in range(KC)]
        for g in range(8):
            w1, kw1 = wg(l, G_M1 + g)
            w18 = k8(w1)
            for oc in range(4):
                fc = g * 4 + oc
                ph = PS()
                mmgroup(psum[:, ph, :], [(w18[:, k, oc * 128:(oc + 1) * 128], mh[:, k, :]) for k in range(KC)], r=[kw1] + mhk, w=[("ps", ph)])
                rl, krl = TB16()
                act(rl[:], psum[:, ph, :], AF.Relu, r=[("ps", ph)], w=[krl])
                tt("pool", pa[:, fc, :], rl[:], rl[:], ALU.mult, [krl], [("pa", fc)])
        pak = [("pa", i) for i in range(32)]
        for dc in range(KC):
            w2, kw2 = wg(l, G_M2 + dc)
            w2v = w2.rearrange("p (k n) -> p k n", k=FC)
            po_ = PS()
            mmgroup(psum[:, po_, :], [(w2v[:, f, :], pa[:, f, :]) for f in range(FC)], r=[kw2] + pak, w=[("ps", po_)])
            stt("dve", xT[:, dc, :], psum[:, po_, :], modT[l][:, 5, dc, col:col + 1], xT[:, dc, :], ALU.mult, ALU.add,
                [("ps", po_), ("modT", l), xkey], [xkey])
        if not (last and lat):
            dma("sp", xtd[ti], xT[:].rearrange("p k t -> p (k t)"), [xkey], [(nm + "xTd", ti)], "xst")
        else:
            for b in range(CPT):
                bankA = psum[:, 12:14, :].rearrange("p a b -> p (a b)")
                bankB = psum[:, 14:16, :].rearrange("p a b -> p (a b)")
                S.add("pe", lambda e, b=b: [e.transpose((bankA if k < 4 else bankB)[:, (k % 4) * 128:(k % 4 + 1) * 128], xT[:, k, b * 128:(b + 1) * 128], ident[:]) for k in range(KC)][-1],
                      r=[xkey, "ident"], w=[("ps", i) for i in (12, 13, 14, 15)])
                full = psum[:, 12:16, :].rearrange("p a b -> p (a b)")
                junk = big32[:].rearrange("p k t -> p (k t)")[:, 0:D]
                ssb, kss = T32()
                S.add("dve", lambda e, ssb=ssb: e.memset(ssb[:, 0:1], 0.0), w=[kss])
                act(junk, full, AF.Square, r=[("ps", i) for i in (12, 13, 14, 15)], w=b32k[0:4] + [kss], accum_out=ssb[:, 0:1])
                l1, kl1 = T32()
                act(l1[:, 0:1], ssb[:, 0:1], AF.Sqrt, r=[kss, "small"], w=[kl1], scale=1.0 / D, bias=eps_a[:])
                rs, krs = T32()
                S.add("dve", lambda e, rs=rs, l1=l1: e.reciprocal(out=rs[:, 0:1], in_=l1[:, 0:1]), r=[kl1], w=[krs])
                stt("dve", xtok[:, b, :], full, rs[:, 0:1], gfin[:], ALU.mult, ALU.mult,
                    [("ps", i) for i in (12, 13, 14, 15)] + [krs, "gfin"], [("xtok", b)])
                dma("sp", out_d[t0 + b * 128:t0 + (b + 1) * 128, :], xtok[:, b, :], [("xtok", b)], [("out", ti, b)], "outst")
        if nxt is not None:
            p2_xload(nxt[0], nxt[1])

    for l in range(n_layers):
        if debug_stage is not None and debug_stage < 2:
            break
        last = l == n_layers - 1
        phase1(l, "ctx", 0)
        if debug_stage is not None and 2 <= debug_stage < 3:
            break
        for ti in range(NT):
            phase1(l, "lat", ti)
        if debug_stage == 3:
            break
        cur_pool[0] = "all"
        seq = [("lat", ti) for ti in range(NT)] + ([] if last else [("ctx", 0)])
        p2_loads(l, seq[0][0], seq[0][1], 0)
        p2_xload(seq[0][0], seq[0][1])
        for i, (kd, ti) in enumerate(seq):
            phase2(l, kd, ti, i, last, seq[i + 1] if i + 1 < len(seq) else None)
        cur_pool[0] = "all"

    S.finalize()

    if os.environ.get("KDUMP"):
        for e in Sched.ENGS:
            print("ENGINE", e, len(S.q[e]))
            for i, op in enumerate(S.q[e][-int(os.environ["KDUMP"]):]):
                print("  ", i, "dma" if op.is_dma else "", op.dkey, "sig" if op.sig else "", op.cnt, op.waits)
    if os.environ.get("KSIM"):
        semv = {}
        pos = {e: 0 for e in Sched.ENGS}
        progress = True
        while progress:
            progress = False
            for e in Sched.ENGS:
                while pos[e] < len(S.q[e]):
                    op = S.q[e][pos[e]]
                    if all(semv.get(k, 0) >= v for k, v in op.waits):
                        if op.is_dma:
                            semv[("d", op.dkey)] = semv.get(("d", op.dkey), 0) + 16
                        elif op.sig:
                            k = ("e", op.eng, (op.cnt - 1) // EPOCH)
                            semv[k] = semv.get(k, 0) + 1
                        pos[e] += 1
                        progress = True
                    else:
                        break
        for e in Sched.ENGS:
            print("SIM", e, pos[e], "/", len(S.q[e]))
            if pos[e] < len(S.q[e]):
                op = S.q[e][pos[e]]
                print("   stuck on waits", [(k, v, semv.get(k, 0)) for k, v in op.waits])
    sems = {}

    def getsem(key):
        s = sems.get(key)
        if s is None:
            s = es.enter_context(nc.semaphore("s%d" % len(sems)))
            sems[key] = s
        return s

    for e in Sched.ENGS:
        for op in S.q[e]:
            for key, val in op.waits:
                getsem(key)
            if op.is_dma:
                getsem(("d", op.dkey))
            elif op.sig:
                getsem(("e", op.eng, (op.cnt - 1) // EPOCH))


    if os.environ.get("KDUMP"):
        print("NSEMS", len(sems), [ (k, getattr(v, "num", None)) for k, v in sems.items()])

    def emit(eng_obj, ops, final_keys=(), throttle=0):
        issued = {}
        for op in ops:
            for key, val in op.waits:
                eng_obj.wait_ge(sems[key], val)
            if throttle and op.is_dma:
                n = issued.get(op.dkey, 0)
                if n > 0 and n % throttle == 0:
                    eng_obj.wait_ge(sems[("d", op.dkey)], 16 * n)
                issued[op.dkey] = n + 1
            ins = op.fn(eng_obj)
            if op.is_dma:
                ins.then_inc(sems[("d", op.dkey)], 16)
            elif op.sig:
                ins.then_inc(sems[("e", op.eng, (op.cnt - 1) // EPOCH)], 1)
        for k in final_keys:
            if k in S.dmacnt:
                eng_obj.wait_ge(sems[("d", k)], 16 * S.dmacnt[k])

    with nc.Block() as block:
        @block.sync
        def _(e):
            emit(e, S.q["sp"])

        @block.tensor
        def _(e):
            emit(e, S.q["pe"])

        @block.scalar
        def _(e):
            emit(e, S.q["act"])

        @block.vector
        def _(e):
            emit(e, S.q["dve"])

        @block.gpsimd
        def _(e):
            emit(e, S.q["pool"], final_keys=list(S.dmacnt.keys()), throttle=int(os.environ.get("THR", "2")))
    es.close()
    return nc


def rope_tables(S_len):
    GRID_W = 64
    t = np.arange(S_len)
    pos = np.stack([t // GRID_W, t % GRID_W], axis=0).astype(np.float32)
    inv_freq = (10000.0 ** (-np.arange(32, dtype=np.float32) * 2.0 / 64.0)).astype(np.float32)
    ang = pos[:, None, :] * inv_freq[None, :, None]
    cos = np.cos(ang).astype(np.float32)
    sin = np.sin(ang).astype(np.float32)
    cosT = np.zeros((128, S_len), np.float32)
    sinT = np.zeros((128, S_len), np.float32)
    for ax in range(2):
        for half in range(2):
            p0 = ax * 64 + half * 32
            cosT[p0:p0 + 32] = cos[ax]
            sinT[p0:p0 + 32] = sin[ax] * (-1.0 if half == 0 else 1.0)
    perm = np.zeros((128, 128), np.float32)
    for m in range(128):
        partner = m + 32 if (m % 64) < 32 else m - 32
        perm[partner, m] = 1.0
    return cosT, sinT, perm


def make_in_maps(inp, S_len, nb):
    f = lambda a: np.ascontiguousarray(np.asarray(a, dtype=np.float32))
    cosT, sinT, perm = rope_tables(S_len)
    pv = np.zeros((L, NPV, D), np.float32)
    pv[:, R_G1] = inp["g_norm1"]
    pv[:, R_G2] = inp["g_norm2"]
    pv[:, R_BCFC] = inp["b_cf_conv"]
    pv[:, R_GLN] = inp["g_cf_ln"]
    pv[:, R_BLN] = inp["b_cf_ln"]
    pv[:, R_BCFO] = inp["b_cf_out"]
    pv[:, R_BMOD:R_BMOD + 6] = np.asarray(inp["b_mod"]).reshape(L, 6, D)
    pv[:, R_WSC:R_WSC + 3] = inp["w_sc_conv"]
    pv[:, R_WCF:R_WCF + 31] = inp["w_cf_conv"]
    qk = np.stack([np.asarray(inp["q_gain"]), np.asarray(inp["k_gain"])], axis=1).astype(np.float32)
    shared = {
        "w_mod": f(inp["w_mod"]), "w_in": f(inp["w_in"]), "w_attn_out": f(inp["w_attn_out"]),
        "w_sc_out": f(inp["w_sc_out"]), "w_cf_out": f(inp["w_cf_out"]), "w_o": f(inp["w_o"]),
        "w_mlp_in": f(inp["w_mlp_in"]), "w_mlp_out": f(inp["w_mlp_out"]), "pv": pv, "qk": f(qk),
        "g_final": f(inp["g_final"]), "rope_cos": cosT, "rope_sin": sinT,
        "ident": np.eye(128, dtype=np.float32), "perm": perm,
    }
    x = np.asarray(inp["x"]); c = np.asarray(inp["c"]); ctx = np.asarray(inp["ctx"]); c_ctx = np.asarray(inp["c_ctx"])
    maps = []
    for b in range(nb):
        m = dict(shared)
        m["x"] = f(x[b])
        m["ctx"] = f(ctx[b])
        m["cc"] = f(np.stack([c[b], c_ctx], axis=0))
        maps.append(m)
    return maps


def kernel(**inputs):
    x = np.asarray(inputs["x"])
    B, S_len, _ = x.shape
    nc = build(S_len)
    in_maps = make_in_maps(inputs, S_len, B)
    res = run_bass_kernel_spmd(nc, in_maps, core_ids=list(range(B)))
    return np.stack([np.asarray(r["out"]) for r in res.results], axis=0).astype(np.float32)
```

```python
import os
import numpy as np
from contextlib import ExitStack
import concourse.bass as bass
import concourse.mybir as mybir
from concourse.bass_utils import run_bass_kernel_spmd

F32 = mybir.dt.float32
BF16 = mybir.dt.bfloat16
AF = mybir.ActivationFunctionType
ALU = mybir.AluOpType

D = 1024
KC = 8
HD = 128
NQH = 8
NKVH = 2
CTXL = 256
DFF = 4096
FC = 32
DIN = 9728
L = 2
TT = 256
NSLOT = 4
NG = 52
NORM_EPS = 1e-6
LN_EPS = 1e-5
EPOCH = 50000
R_G1, R_G2, R_BCFC, R_GLN, R_BLN, R_BCFO, R_BMOD, R_WSC, R_WCF, NPV = 0, 1, 2, 3, 4, 5, 6, 12, 15, 46
G_Q, G_KV, G_SCB, G_SCC, G_SCH, G_CFA, G_CFG, G_GA, G_GB, G_GC = 0, 2, 3, 5, 7, 9, 11, 13, 15, 17
G_AO, G_SO, G_CO, G_WO, G_M1, G_M2, G_DCF, G_DSC = 19, 21, 23, 25, 27, 35, 43, 51


class Op:
    __slots__ = ("eng", "fn", "deps", "sig", "cnt", "dkey", "dval", "is_dma", "waits")


class Sched:
    ENGS = ("pe", "act", "dve", "pool", "sp")

    def __init__(self):
        self.q = {e: [] for e in self.ENGS}
        self.lastw = {}
        self.rd = {}
        self.dmacnt = {}

    def add(self, eng, fn, r=(), w=(), dma=None):
        r = [("ps", k[1] // 2) if (isinstance(k, tuple) and k[0] == "ps") else k for k in r]
        w = [("ps", k[1] // 2) if (isinstance(k, tuple) and k[0] == "ps") else k for k in w]
        op = Op()
        op.eng = eng
        op.fn = fn
        op.is_dma = dma is not None
        op.sig = False
        op.cnt = 0
        op.dkey = dma
        op.dval = 0
        if dma is not None:
            n = self.dmacnt.get(dma, 0) + 1
            self.dmacnt[dma] = n
            op.dval = 16 * n
        deps = {}
        lastw = self.lastw
        rd = self.rd

        def consider(d, raw):
            if d is op:
                return
            if (not d.is_dma) and (not op.is_dma) and d.eng == eng and not raw and eng == "pe":
                return
            deps[id(d)] = d

        for k in r:
            d = lastw.get(k)
            if d is not None:
                consider(d, True)
        for k in w:
            d = lastw.get(k)
            if d is not None:
                consider(d, False)
            rr = rd.get(k)
            if rr:
                for e2, o in rr.items():
                    if e2 == "dma":
                        for o2 in o:
                            consider(o2, False)
                    else:
                        consider(o, False)
        for k in r:
            rr = rd.get(k)
            if rr is None:
                rr = rd[k] = {}
            if op.is_dma:
                rr.setdefault("dma", []).append(op)
            else:
                rr[eng] = op
        for k in w:
            lastw[k] = op
            rd[k] = {}
        op.deps = []
        op.waits = {}
        for d in deps.values():
            if d.is_dma:
                key = ("d", d.dkey)
                val = 16 * self.dmacnt[d.dkey]
                if op.dkey == d.dkey:
                    val -= 16
                if op.waits.get(key, 0) < val:
                    op.waits[key] = val
            else:
                d.sig = True
                op.deps.append(d)
        self.q[eng].append(op)
        return op

    def finalize(self):
        self.nsig = {}
        for e in self.ENGS:
            c = 0
            for op in self.q[e]:
                if op.sig and not op.is_dma:
                    c += 1
                    op.cnt = c
            self.nsig[e] = c
        for e in self.ENGS:
            seen = {}
            for op in self.q[e]:
                waits = op.waits
                for d in op.deps:
                    ep = (d.cnt - 1) // EPOCH
                    key = ("e", d.eng, ep)
                    val = d.cnt - ep * EPOCH
                    if waits.get(key, 0) < val:
                        waits[key] = val
                op.waits = []
                for key, val in waits.items():
                    if seen.get(key, 0) >= val:
                        continue
                    seen[key] = val
                    op.waits.append((key, val))
                op.deps = None


def build(S_len, n_layers=L, debug_stage=None):
    nc = bass.Bass("TRN2", target_bir_lowering=False)
    T = TT
    assert T == CTXL and S_len % T == 0
    NT = S_len // T
    NKEY = CTXL + S_len
    NCH = NKEY // 128
    CPT = T // 128

    def din(name, shape, dt=F32):
        return nc.dram_tensor(name, list(shape), dt, kind="ExternalInput").ap()

    def dscr(name, shape, dt):
        return nc.dram_tensor(name, list(shape), dt, kind="Internal").ap()

    x_d = din("x", [S_len, D])
    ctx_d = din("ctx", [CTXL, D])
    cc_d = din("cc", [2, D])
    wmod_d = din("w_mod", [L, D, 6 * D])
    win_d = din("w_in", [L, D, DIN])
    wsq_d = [din(n, [L, D, D]) for n in ("w_attn_out", "w_sc_out", "w_cf_out", "w_o")]
    wm1_d = din("w_mlp_in", [L, D, DFF])
    wm2_d = din("w_mlp_out", [L, DFF, D])
    pv_d = din("pv", [L, NPV, D])
    qk_d = din("qk", [L, 2, HD])
    gfin_d = din("g_final", [D])
    cos_d = din("rope_cos", [128, S_len])
    sin_d = din("rope_sin", [128, S_len])
    ident_d = din("ident", [128, 128])
    perm_d = din("perm", [128, 128])
    out_d = nc.dram_tensor("out", [S_len, D], F32, kind="ExternalOutput").ap()

    xTd = dscr("xTd", [NT, 128, KC * T], F32)
    hTd = dscr("hTd", [NT, 128, KC * T], BF16)
    uTd = dscr("uTd", [KC, 128, S_len + 30], BF16)
    zTd = dscr("zTd", [KC, 128, S_len + 2], BF16)
    cxTd = dscr("cxTd", [2, 128, KC * T], F32)
    chTd = dscr("chTd", [2, 128, KC * T], BF16)
    cuTd = dscr("cuTd", [KC, 128, CTXL + 30], BF16)
    czTd = dscr("czTd", [KC, 128, CTXL + 2], BF16)
    wbf = dscr("wbf", [L, NG, 128, 4096], BF16)

    S = Sched()
    es = ExitStack()

    def sb(name, shape, dt):
        return es.enter_context(nc.sbuf_tensor("sb_" + name, list(shape), dt))

    Kres = sb("Kres", [128, NKVH, NKEY], BF16)
    Vres = sb("Vres", [128, NCH, NKVH * HD], BF16)
    wslot = [sb(f"ws{i}", [128, 4096], BF16) for i in range(NSLOT)]
    xT = sb("xT", [128, KC, T], F32)
    hT = [sb(f"hT{i}", [128, KC, T], BF16) for i in range(2)]
    uwin = sb("uwin", [128, KC, T + 30], BF16)
    zwin = sb("zwin", [128, KC, T + 2], BF16)
    cs = [sb(f"cs{i}", [128, 2, T], F32) for i in range(2)]
    pa = sb("pa", [128, 32, T], BF16)
    big32 = sb("big32", [128, KC, T], F32)
    mh = sb("mh", [128, KC, T], BF16)
    NPT = 4
    pt = [sb(f"pt{i}", [128, 4, T], BF16) for i in range(NPT)]
    xtok = sb("xtok", [128, 2, D], F32)
    sq = sb("sq", [128, KC, T], BF16)
    NTMP = 7
    tmp = [sb(f"tmp{i}", [128, T], F32) for i in range(NTMP)]
    tmpb = [sb(f"tmpb{i}", [128, T], BF16) for i in range(4)]
    qnb = [sb(f"qnb{i}", [128, T], BF16) for i in range(4)]
    ps2b = [sb(f"ps2b{i}", [128, 2, T], BF16) for i in range(2)]
    ps1b = [sb(f"ps1b{i}", [128, T], BF16) for i in range(2)]
    ident = sb("ident", [128, 128], F32)
    identb = sb("identb", [128, 128], BF16)
    permf = sb("permf", [128, 128], F32)
    permb = sb("permb", [128, 128], BF16)
    onesb = sb("onesb", [128, 128], BF16)
    zeros = sb("zeros", [128, KC, 16], BF16)
    pvrow = sb("pvrow", [128, D], F32)
    ccrow = sb("ccrow", [128, D], F32)
    qkrow = sb("qkrow", [128, HD], F32)
    gfin = sb("gfin", [128, D], F32)
    scT = sb("scT", [128, KC, 2], F32)
    pvT = [sb(f"pvT{l}", [128, KC, NPV], F32) for l in range(L)]
    qkT = [sb(f"qkT{l}", [128, 2], F32) for l in range(L)]
    modT = [sb(f"modT{l}", [128, 6, KC, 2], F32) for l in range(L)]
    avec = [sb(f"avec{l}", [128, 2, 2, KC], F32) for l in range(L)]
    lnv = [sb(f"lnv{l}", [128, 4, KC], F32) for l in range(L)]
    small = sb("small", [128, 8], F32)
    eps_a = sb("eps_a", [128, 1], F32)
    eps_b = sb("eps_b", [128, 1], F32)
    eps_c = sb("eps_c", [128, 1], F32)
    psum = es.enter_context(nc.psum_tensor("psum", [128, 16, 256], F32))

    if os.environ.get("KDUMP"):
        print("SBUF remaining", nc.sbuf_bytes_remaining)
    tmp_i = [0]

    def T32():
        i = tmp_i[0] % NTMP
        tmp_i[0] += 1
        return tmp[i], ("tmp", i)

    tmpb_i = [0]

    def TB16():
        i = tmpb_i[0] % 4
        tmpb_i[0] += 1
        return tmpb[i], ("tmpb", i)

    ps_rot = {"gen": [12, 14], "all": [0, 2, 4, 6, 8, 10, 12, 14]}
    ps_idx = {"gen": 0, "all": 0}

    cur_pool = ["all"]

    def PS(pool=None):
        pool = pool or cur_pool[0]
        lst = ps_rot[pool]
        i = lst[ps_idx[pool] % len(lst)]
        ps_idx[pool] += 1
        return i

    wcount = [0]

    def wload(src_ap, rkeys, f32=False, ncol=4096):
        s = wcount[0] % NSLOT
        wcount[0] += 1
        dst = wslot[s][:, 0:ncol]
        if f32:
            dst = dst.bitcast(F32)
        S.add("sp", lambda e, dst=dst, src=src_ap: e.dma_start(out=dst, in_=src),
              r=rkeys, w=[("ws", s)], dma=("ws", s))
        return wslot[s], ("ws", s)

    def wg(l, g):
        rk = [("wbf", l, g)] if not (G_M2 <= g < G_M2 + 8) else [("wbf", l, g, q4) for q4 in range(4)]
        ncol = 3072 if g == G_DSC else (3968 if g >= G_DCF else 4096)
        slot, key = wload(wbf[l, g, :, 0:ncol], rk, ncol=ncol)
        return slot[:], key

    def k8(slot_ap, n=512):
        return slot_ap.rearrange("p (k n) -> p k n", k=KC)

    def mmgroup(ps_ap, pairs, r, w, start=True, stop=True):
        n = len(pairs)

        def fn(e):
            ins = None
            for i, (a, b) in enumerate(pairs):
                ins = e.matmul(ps_ap, a, b, start=(start and i == 0), stop=(stop and i == n - 1))
            return ins
        S.add("pe", fn, r=r, w=w)

    def act(out, in_, func, r, w, bias=None, scale=None, accum_out=None):
        kw = {}
        if bias is not None:
            kw["bias"] = bias
        if scale is not None:
            kw["scale"] = scale
        if accum_out is not None:
            kw["accum_out"] = accum_out
        S.add("act", lambda e: e.activation(out=out, in_=in_, func=func, **kw), r=r, w=w)

    def tt(eng, out, in0, in1, op, r, w):
        S.add(eng, lambda e: e.tensor_tensor(out=out, in0=in0, in1=in1, op=op), r=r, w=w)

    def ts(eng, out, in0, s1, s2, op0, op1, r, w):
        if s2 is None:
            S.add(eng, lambda e: e.tensor_scalar(out=out, in0=in0, scalar1=s1, scalar2=None, op0=op0), r=r, w=w)
        else:
            S.add(eng, lambda e: e.tensor_scalar(out=out, in0=in0, scalar1=s1, scalar2=s2, op0=op0, op1=op1), r=r, w=w)

    def stt(eng, out, in0, scalar, in1, op0, op1, r, w):
        S.add(eng, lambda e: e.scalar_tensor_tensor(out=out, in0=in0, scalar=scalar, in1=in1, op0=op0, op1=op1), r=r, w=w)

    def cp(eng, out, in_, r, w):
        if eng == "act":
            S.add("act", lambda e: e.activation(out=out, in_=in_, func=AF.Copy), r=r, w=w)
        else:
            S.add(eng, lambda e: e.tensor_copy(out=out, in_=in_), r=r, w=w)

    def dma(q, out, in_, r, w, key, slow=False):
        if slow:
            S.add(q, lambda e: e.dma_start(out=out, in_=in_, allow_slow_non_contiguous=True), r=r, w=w, dma=key)
        else:
            S.add(q, lambda e: e.dma_start(out=out, in_=in_), r=r, w=w, dma=key)

    def rsqrt_from(ps_or_sb, rkeys, scale, eps):
        t1, k1 = T32()
        act(t1[:], ps_or_sb, AF.Sqrt, r=rkeys, w=[k1], scale=scale, bias=eps_ap(eps))
        t2, k2 = T32()
        S.add("dve", lambda e, t1=t1, t2=t2: e.reciprocal(out=t2[:], in_=t1[:]), r=[k1], w=[k2])
        return t2, k2

    eps_tiles = {}

    def eps_ap(v):
        return eps_tiles[v]

    S.add("dve", lambda e: e.memset(onesb[:], 1.0), w=["onesb"])
    S.add("dve", lambda e: e.memset(zeros[:], 0.0), w=["zeros"])
    S.add("dve", lambda e: e.memset(eps_a[:], NORM_EPS), w=["small"])
    S.add("dve", lambda e: e.memset(eps_b[:], LN_EPS), w=["small"])
    S.add("dve", lambda e: e.memset(eps_c[:], 128.0 * NORM_EPS), w=["small"])
    eps_tiles[NORM_EPS] = eps_a[:]
    eps_tiles[LN_EPS] = eps_b[:]
    dma("act", ident[:], ident_d, [], ["ident"], "c_ident")
    dma("act", permf[:], perm_d, [], ["permf"], "c_perm")
    dma("act", ccrow[0:2, :], cc_d, [], ["ccrow"], "c_cc")
    dma("act", gfin[:], gfin_d.partition_broadcast(128), [], ["gfin"], "c_gfin")
    cp("dve", identb[:], ident[:], ["ident"], ["identb"])
    cp("dve", permb[:], permf[:], ["permf"], ["permb"])
    for (td, n, padw, nm) in ((uTd, S_len, 15, "luTd"), (zTd, S_len, 1, "lzTd"), (cuTd, CTXL, 15, "cuTd"), (czTd, CTXL, 1, "czTd")):
        dma("pool", td[:, :, 0:padw].rearrange("k p t -> p k t"), zeros[:, :, 0:padw], ["zeros"], [(nm, "padl")], "c_pad", slow=True)
        dma("pool", td[:, :, padw + n:padw + n + padw].rearrange("k p t -> p k t"), zeros[:, :, 0:padw], ["zeros"], [(nm, "padr")], "c_pad", slow=True)

    def cast_group(l, g):
        dst = wbf[l, g]
        if g < 19:
            src = win_d[l, :, g * 512:(g + 1) * 512].rearrange("(k p) n -> p k n", p=128)
            dv = dst.rearrange("p (k n) -> p k n", k=KC)
        elif g < 27:
            wi, h = (g - 19) // 2, (g - 19) % 2
            src = wsq_d[wi][l, :, h * 512:(h + 1) * 512].rearrange("(k p) n -> p k n", p=128)
            dv = dst.rearrange("p (k n) -> p k n", k=KC)
        elif g < 35:
            src = wm1_d[l, :, (g - 27) * 512:(g - 26) * 512].rearrange("(k p) n -> p k n", p=128)
            dv = dst.rearrange("p (k n) -> p k n", k=KC)
        else:
            dc = g - 35
            src = wm2_d[l, :, dc * 128:(dc + 1) * 128].rearrange("(k p) n -> p k n", p=128)
            dv = dst.rearrange("p (k n) -> p k n", k=FC)
            for q4 in range(4):
                dma("pool", dv[:, q4 * 8:(q4 + 1) * 8, :], src[:, q4 * 8:(q4 + 1) * 8, :], [], [("wbf", l, g, q4)], ("cast", l, 2))
            return
        dma("pool", dv, src, [], [("wbf", l, g)], ("cast", l, 0 if g in (2, 5, 6, 7, 8, 9, 10, 11, 12) else 1))

    cast_order = [2, 5, 6, 7, 8, 9, 10, 11, 12, 0, 1, 3, 4] + list(range(13, 43))
    for l in range(n_layers):
        for g in cast_order:
            cast_group(l, g)

    act(ccrow[0:2, :], ccrow[0:2, :], AF.Silu, r=["ccrow"], w=["ccrow"])
    S.add("pe", lambda e: [e.transpose(psum[:, 0, 2 * k:2 * k + 2], ccrow[0:2, k * 128:(k + 1) * 128], ident[0:2, 0:2]) for k in range(KC)][-1],
          r=["ccrow", "ident"], w=[("ps", 0)])
    cp("dve", scT[:].rearrange("p k c -> p (k c)"), psum[:, 0, 0:16], [("ps", 0)], ["scT"])

    def prep(l):
        dma("act", pvrow[0:NPV, :], pv_d[l], [], ["pvrow"], "c_pv")
        dma("act", qkrow[0:2, :], qk_d[l], [], ["qkrow"], "c_qk")
        bank = psum[:, 0:2, :].rearrange("p a b -> p (a b)")
        S.add("pe", lambda e: [e.transpose(bank[:, k * NPV:(k + 1) * NPV], pvrow[0:NPV, k * 128:(k + 1) * 128], ident[0:NPV, 0:NPV]) for k in range(KC)][-1],
              r=["pvrow", "ident"], w=[("ps", 0), ("ps", 1)])
        cp("dve", pvT[l][:].rearrange("p k r -> p (k r)"), bank[:, 0:KC * NPV], [("ps", 0), ("ps", 1)], [("pvT", l)])
        S.add("pe", lambda e: e.transpose(psum[:, 2, 0:2], qkrow[0:2, :], ident[0:2, 0:2]), r=["qkrow", "ident"], w=[("ps", 2)])
        ts("dve", qkT[l][:], psum[:, 2, 0:2], float(np.sqrt(128.0)), None, ALU.mult, None, [("ps", 2)], [("qkT", l)])
        if debug_stage == 1.1:
            return
        mps = 3
        for j2 in range(24):
            src = wmod_d[l, :, j2 * 256:(j2 + 1) * 256].rearrange("(k p) n -> p k n", p=128)
            s = wcount[0] % NSLOT
            wcount[0] += 1
            dstv = wslot[s][:].bitcast(F32).rearrange("p (k n) -> p k n", k=KC)
            S.add("sp", lambda e, dstv=dstv, src=src: e.dma_start(out=dstv, in_=src), r=[], w=[("ws", s)], dma=("ws", s))
            for jj in range(2):
                j = j2 * 2 + jj
                mmgroup(psum[:, mps, 2 * j:2 * j + 2],
                        [(dstv[:, k, jj * 128:(jj + 1) * 128], scT[:, k, :]) for k in range(KC)],
                        r=[("ws", s), "scT"], w=[("ps", mps)])
        if debug_stage == 1.2:
            return
        for sel in range(6):
            tt("dve", modT[l][:, sel, :, :], psum[:, mps, sel * 16:(sel + 1) * 16].rearrange("p (k c) -> p k c", c=2),
               pvT[l][:, :, R_BMOD + sel:R_BMOD + sel + 1].to_broadcast([128, KC, 2]), ALU.add,
               [("ps", mps), ("pvT", l)], [("modT", l)])
        for col in range(2):
            stt("dve", avec[l][:, 0, col, :], modT[l][:, 1, :, col], 1.0, pvT[l][:, :, R_G1], ALU.add, ALU.mult,
                [("modT", l), ("pvT", l)], [("avec", l)])
            stt("dve", avec[l][:, 1, col, :], modT[l][:, 4, :, col], 1.0, pvT[l][:, :, R_G2], ALU.add, ALU.mult,
                [("modT", l), ("pvT", l)], [("avec", l)])
        for i, rr in enumerate((R_GLN, R_BLN, R_BCFC, R_BCFO)):
            cp("dve", lnv[l][:, i, :], pvT[l][:, :, rr], [("pvT", l)], [("lnv", l)])
        if debug_stage == 1.3:
            return
        stg = pa[:, 0:16, :].rearrange("p a b -> p (a b)")
        stgk = [("pa", i) for i in range(16)]
        for j in range(KC):
            for k in range(31):
                ts("dve", stg[:, k * 128:(k + 1) * 128], identb[:], pvT[l][:, j, R_WCF + k:R_WCF + k + 1], None, ALU.mult, None,
                   ["identb", ("pvT", l)], stgk)
            dma("pool", wbf[l, G_DCF + j, :, 0:31 * 128], stg[:, 0:31 * 128], stgk, [("wbf", l, G_DCF + j)], ("dg", l))
        for j in range(KC):
            for k in range(3):
                ts("dve", stg[:, (j * 3 + k) * 128:(j * 3 + k + 1) * 128], identb[:], pvT[l][:, j, R_WSC + k:R_WSC + k + 1], None, ALU.mult, None,
                   ["identb", ("pvT", l)], stgk)
        dma("pool", wbf[l, G_DSC, :, 0:3072], stg[:, 0:3072], stgk, [("wbf", l, G_DSC)], ("dg", l))

    if debug_stage is None or debug_stage >= 1:
        for l in range(n_layers):
            prep(l)

    sqk = [("sq", k) for k in range(KC)]
    b32k = [("big32", k) for k in range(KC)]
    xtk = [("xtok", 0), ("xtok", 1)]

    def headnorm(l, psi, gain_ap, rope, cs_t, cs_key, dst_ap, dst_keys):
        sqb, sqk = TB16()
        act(sqb[:], psum[:, psi, :], AF.Square, r=[("ps", psi)], w=[sqk])
        HN = int(os.environ.get("HN", "9"))
        if HN < 1:
            return
        p2 = PS()
        mmgroup(psum[:, p2, :], [(onesb[:], sqb[:])], r=["onesb", sqk], w=[("ps", p2)])
        if HN < 2:
            return
        t1, k1 = T32()
        act(t1[:], psum[:, p2, :], AF.Sqrt, r=[("ps", p2), "small"], w=[k1], bias=eps_c[:], scale=1.0)
        rstd, kr = T32()
        S.add("dve", lambda e, rstd=rstd, t1=t1: e.reciprocal(out=rstd[:], in_=t1[:]), r=[k1], w=[kr])
        tg, ktg = T32()
        ts("dve", tg[:], psum[:, psi, :], gain_ap, None, ALU.mult, None, [("ps", psi), ("qkT", l)], [ktg])
        if not rope:
            tt("dve", dst_ap, tg[:], rstd[:], ALU.mult, [ktg, kr], dst_keys)
            return
        qn, kq = TB16()
        tt("dve", qn[:], tg[:], rstd[:], ALU.mult, [ktg, kr], [kq])
        p3 = PS()
        mmgroup(psum[:, p3, :], [(permb[:], qn[:])], r=["permb", kq], w=[("ps", p3)])
        ta, ka = T32()
        tt("pool", ta[:], qn[:], cs_t[:, 0, :], ALU.mult, [kq, cs_key], [ka])
        tb, kb = T32()
        tt("dve", tb[:], psum[:, p3, :], cs_t[:, 1, :], ALU.mult, [("ps", p3), cs_key], [kb])
        tt("dve", dst_ap, ta[:], tb[:], ALU.add, [ka, kb], dst_keys)

    def phase1(l, kind, ti):
        lat = kind == "lat"
        col = 0 if lat else 1
        t0 = ti * T
        kch0 = (CTXL // 128 + ti * CPT) if lat else 0
        keyoff = CTXL + t0 if lat else 0
        xsrc_tok = x_d if lat else ctx_d
        xtd = xTd if lat else cxTd
        htd = hTd if lat else chTd
        utd = uTd if lat else cuTd
        ztd = zTd if lat else czTd
        nm = "l" if lat else "c"
        xkey = "xT"
        hbuf = hT[ti % 2]
        hkey = ("hT", ti % 2)
        if l == 0:
            for b in range(2):
                dma("act", xtok[:, b, :], xsrc_tok[t0 + b * 128:t0 + (b + 1) * 128, :], [], [("xtok", b)], "xtok")
            for k in range(KC):
                pi = PS("all")
                S.add("pe", lambda e, pi=pi, k=k: [e.transpose(psum[:, pi, b * 128:(b + 1) * 128], xtok[:, b, k * 128:(k + 1) * 128], ident[:]) for b in range(2)][-1],
                      r=xtk + ["ident"], w=[("ps", pi)])
                cp("act" if k % 2 else "dve", xT[:, k, :], psum[:, pi, :], [("ps", pi)], [xkey])
            if debug_stage != 2.05:
                _v = os.environ.get("XV", "")
                _src = big32 if _v == "big32" else xT
                _dst = xTd[0] if _v == "xTd" else xtd[ti]
                dma("sp", _dst, _src[:].rearrange("p k t -> p (k t)"), [xkey], [(nm + "xTd", ti)], "xst")
        else:
            dma("act", xT[:].rearrange("p k t -> p (k t)"), xtd[ti], [(nm + "xTd", ti)], [xkey], "xld")
        if lat:
            cst = cs[ti % 2]
            cskey = ("cs", ti % 2)
            dma("act", cst[:, 0, :], cos_d[:, t0:t0 + T], [], [cskey], cskey)
            dma("act", cst[:, 1, :], sin_d[:, t0:t0 + T], [], [cskey], cskey)
        else:
            cst, cskey = None, None
        if debug_stage == 2.05 and os.environ.get("XV") == "dummy":
            dma("act", xtok[:], xsrc_tok[t0:t0 + T, :].rearrange("(b p) d -> p b d", p=128), [], xtk, "xtok")
        if debug_stage in (2.1, 2.05):
            return
        act(sq[:].rearrange("p k t -> p (k t)"), xT[:].rearrange("p k t -> p (k t)"), AF.Square, r=[xkey], w=sqk)
        pss = PS("all")
        mmgroup(psum[:, pss, :], [(onesb[:], sq[:, k, :]) for k in range(KC)], r=["onesb"] + sqk, w=[("ps", pss)])
        rstd, kr = rsqrt_from(psum[:, pss, :], [("ps", pss), "small"], 1.0 / D, NORM_EPS)
        tt("dve", big32[:], xT[:], rstd[:].unsqueeze(1).to_broadcast([128, KC, T]), ALU.mult, [xkey, kr], b32k)
        for k in range(KC):
            ts("dve", hbuf[:, k, :], big32[:, k, :], avec[l][:, 0, col, k:k + 1], modT[l][:, 0, k, col:col + 1],
               ALU.mult, ALU.add, [("big32", k), ("avec", l), ("modT", l)], [hkey])
        dma("sp", htd[ti], hbuf[:].rearrange("p k t -> p (k t)"), [hkey], [(nm + "hTd", ti)], "hst")
        if debug_stage == 2.2:
            return
        wkv, kkv = wg(l, G_KV)
        wkv8 = k8(wkv)
        for j in range(NKVH):
            pi = PS("all")
            mmgroup(psum[:, pi, :], [(wkv8[:, k, j * 128:(j + 1) * 128], hbuf[:, k, :]) for k in range(KC)],
                    r=[kkv, hkey], w=[("ps", pi)])
            if debug_stage == 2.21:
                continue
            headnorm(l, pi, qkT[l][:, 1:2], lat, cst, cskey, Kres[:, j, keyoff:keyoff + T],
                     [("K", kch0 + c) for c in range(CPT)])
        if debug_stage in (2.21, 2.22):
            return
        for b in range(CPT):
            pi = PS("all")
            mmgroup(psum[:, pi, :], [(hbuf[:, k, b * 128:(b + 1) * 128], wkv8[:, k, 256:512]) for k in range(KC)],
                    r=[kkv, hkey], w=[("ps", pi)])
            cp("act", Vres[:, kch0 + b, :], psum[:, pi, :], [("ps", pi)], [("V", kch0 + b)])
        if debug_stage == 2.3:
            return
        for half in range(2):
            wa, ka_ = wg(l, G_CFA + half)
            wgt, kg_ = wg(l, G_CFG + half)
            wa8, wg8 = k8(wa), k8(wgt)
            for oc in range(4):
                j = half * 4 + oc
                pa_i = PS("all")
                pg_i = PS("all")
                mmgroup(psum[:, pa_i, :], [(wa8[:, k, oc * 128:(oc + 1) * 128], hbuf[:, k, :]) for k in range(KC)], r=[ka_, hkey], w=[("ps", pa_i)])
                mmgroup(psum[:, pg_i, :], [(wg8[:, k, oc * 128:(oc + 1) * 128], hbuf[:, k, :]) for k in range(KC)], r=[kg_, hkey], w=[("ps", pg_i)])
                sg, ksg = T32()
                act(sg[:], psum[:, pg_i, :], AF.Sigmoid, r=[("ps", pg_i)], w=[ksg])
                tt("dve", pa[:, 24 + j, :], psum[:, pa_i, :], sg[:], ALU.mult, [("ps", pa_i), ksg], [("pa", 24 + j)])
        dma("sp", utd[:, :, 15 + t0:15 + t0 + T].rearrange("k p t -> p k t"), pa[:, 24:32, :],
            [("pa", 24 + j) for j in range(KC)], [(nm + "uTd", ti)], "ust")
        if debug_stage == 2.4:
            return
        for half in range(2):
            wc, kc_ = wg(l, G_SCC + half)
            wh, kh_ = wg(l, G_SCH + half)
            wc8, wh8 = k8(wc), k8(wh)
            for oc in range(4):
                j = half * 4 + oc
                pc_i = PS("all")
                ph_i = PS("all")
                mmgroup(psum[:, pc_i, :], [(wc8[:, k, oc * 128:(oc + 1) * 128], hbuf[:, k, :]) for k in range(KC)], r=[kc_, hkey], w=[("ps", pc_i)])
                mmgroup(psum[:, ph_i, :], [(wh8[:, k, oc * 128:(oc + 1) * 128], hbuf[:, k, :]) for k in range(KC)], r=[kh_, hkey], w=[("ps", ph_i)])
                cg, kcg = T32()
                cp("act", cg[:], psum[:, pc_i, :], [("ps", pc_i)], [kcg])
                tt("dve", pa[:, 16 + j, :], psum[:, ph_i, :], cg[:], ALU.mult, [("ps", ph_i), kcg], [("pa", 16 + j)])
        dma("sp", ztd[:, :, 1 + t0:1 + t0 + T].rearrange("k p t -> p k t"), pa[:, 16:24, :],
            [("pa", 16 + j) for j in range(KC)], [(nm + "zTd", ti)], "zst")

    def p2_loads(l, kind, ti, hi):
        lat = kind == "lat"
        t0 = ti * T
        ntile = NT if lat else 1
        htd = hTd if lat else chTd
        utd = uTd if lat else cuTd
        ztd = zTd if lat else czTd
        nm = "l" if lat else "c"
        hbuf = hT[hi % 2]
        hkey = ("hT", hi % 2)
        dma("act", hbuf[:].rearrange("p k t -> p (k t)"), htd[ti], [(nm + "hTd", ti)], [hkey], hkey)
        if lat:
            cst = cs[hi % 2]
            cskey = ("cs", hi % 2)
            dma("act", cst[:, 0, :], cos_d[:, t0:t0 + T], [], [cskey], cskey)
            dma("act", cst[:, 1, :], sin_d[:, t0:t0 + T], [], [cskey], cskey)
        nbr = [(nm + "uTd", i) for i in (ti - 1, ti, ti + 1) if 0 <= i < ntile] + [(nm + "uTd", "padl"), (nm + "uTd", "padr")]
        dma("act", uwin[:], utd[:, :, t0:t0 + T + 30].rearrange("k p t -> p k t"), nbr, ["uwin"], "uwin")
        nbr = [(nm + "zTd", i) for i in (ti - 1, ti, ti + 1) if 0 <= i < ntile] + [(nm + "zTd", "padl"), (nm + "zTd", "padr")]
        dma("act", zwin[:], ztd[:, :, t0:t0 + T + 2].rearrange("k p t -> p k t"), nbr, ["zwin"], "zwin")

    def p2_xload(kind, ti):
        lat = kind == "lat"
        xtd = xTd if lat else cxTd
        nm = "l" if lat else "c"
        dma("act", xT[:].rearrange("p k t -> p (k t)"), xtd[ti], [(nm + "xTd", ti)], ["xT"], "xld")

    def phase2(l, kind, ti, hi, last, nxt):
        lat = kind == "lat"
        col = 0 if lat else 1
        t0 = ti * T
        xtd = xTd if lat else cxTd
        nm = "l" if lat else "c"
        xkey = "xT"
        hbuf = hT[hi % 2]
        hkey = ("hT", hi % 2)
        if lat:
            cst = cs[hi % 2]
            cskey = ("cs", hi % 2)
        else:
            cst, cskey = None, None
        cur_pool[0] = "gen"
        wqs = {}
        gain_ap = qkT[l][:, 0:1]
        st = {}

        def qs0(h):
            if h % 4 == 0:
                wqs[h // 4] = wg(l, G_Q + h // 4)
            wq, kq_ = wqs[h // 4]
            oc = h % 4
            pi = (h % 4) * 2
            mmgroup(psum[:, pi, :], [(k8(wq)[:, k, oc * 128:(oc + 1) * 128], hbuf[:, k, :]) for k in range(KC)], r=[kq_, hkey], w=[("ps", pi)])
            st[h] = {"pi": pi}

        def qs1(h):
            pi = st[h]["pi"]
            sqb, sqk_ = TB16()
            act(sqb[:], psum[:, pi, :], AF.Square, r=[("ps", pi)], w=[sqk_])
            p2 = 8 + (h % 2) * 2
            mmgroup(psum[:, p2, :], [(onesb[:], sqb[:])], r=["onesb", sqk_], w=[("ps", p2)])
            st[h]["p2"] = p2

        def qs2(h):
            pi, p2 = st[h]["pi"], st[h]["p2"]
            t1, k1 = T32()
            act(t1[:], psum[:, p2, :], AF.Sqrt, r=[("ps", p2), "small"], w=[k1], bias=eps_c[:], scale=1.0)
            rstd, kr = T32()
            S.add("dve", lambda e, rstd=rstd, t1=t1: e.reciprocal(out=rstd[:], in_=t1[:]), r=[k1], w=[kr])
            tg, ktg = T32()
            ts("dve", tg[:], psum[:, pi, :], gain_ap, None, ALU.mult, None, [("ps", pi), ("qkT", l)], [ktg])
            if not lat:
                tt("dve", pa[:, h, :], tg[:], rstd[:], ALU.mult, [ktg, kr], [("pa", h)])
                return
            qn = qnb[h % 4]
            tt("dve", qn[:], tg[:], rstd[:], ALU.mult, [ktg, kr], [("qnb", h % 4)])

        def qs3(h):
            if not lat:
                return
            qn = qnb[h % 4]
            kq = ("qnb", h % 4)
            p3 = 12 + (h % 2) * 2
            mmgroup(psum[:, p3, :], [(permb[:], qn[:])], r=["permb", kq], w=[("ps", p3)])
            ta, ka = T32()
            tt("pool", ta[:], qn[:], cst[:, 0, :], ALU.mult, [kq, cskey], [ka])
            tb, kb = T32()
            tt("dve", tb[:], psum[:, p3, :], cst[:, 1, :], ALU.mult, [("ps", p3), cskey], [kb])
            tt("dve", pa[:, h, :], ta[:], tb[:], ALU.add, [ka, kb], [("pa", h)])

        if os.environ.get("SKEW", "1") == "1":
            for step in range(NQH + 3):
                if step < NQH:
                    qs0(step)
                if 0 <= step - 1 < NQH:
                    qs1(step - 1)
                if 0 <= step - 2 < NQH:
                    qs2(step - 2)
                if 0 <= step - 3 < NQH:
                    qs3(step - 3)
        else:
            for h in range(NQH):
                qs0(h)
                qs1(h)
                qs2(h)
                qs3(h)
        chunks = list(range(NCH)) if lat else list(range(CTXL // 128))
        groups = [chunks[i:i + 4] for i in range(0, len(chunks), 4)]
        ngr = len(groups)
        for j in range(NQH):
            kv = j // (NQH // NKVH)
            po, psm = 8, 10
            qap = pa[:, j, :]

            def emit_S(gi, j=j, kv=kv, qap=qap):
                grp = groups[gi]
                base = (0, 4, 12)[gi % 3]
                for ci, c in enumerate(grp):
                    mmgroup(psum[:, base + ci, :], [(Kres[:, kv, c * 128:(c + 1) * 128], qap)],
                            r=[("K", c), ("pa", j)], w=[("ps", base + ci)])
                n = len(grp)
                act(pt[gi % NPT][:, 0:n, :], psum[:, base:base + n, :], AF.Exp,
                    r=[("ps", base + ci) for ci in range(n)], w=[("pt", gi % NPT)], scale=float(HD ** -0.5))

            def emit_PV(gi, j=j, kv=kv, po=po, psm=psm):
                grp = groups[gi]
                n = len(grp)
                ptb = pt[gi % NPT]
                s1 = ps1b[gi % 2]
                OSUM = os.environ.get("OSUM", "1") == "1"
                if not OSUM:
                    pass
                elif n == 4:
                    s2 = ps2b[gi % 2]
                    tt("dve", s2[:], ptb[:, 0:2, :], ptb[:, 2:4, :], ALU.add, [("pt", gi % NPT)], [("ps2b", gi % 2)])
                    tt("dve", s1[:], s2[:, 0, :], s2[:, 1, :], ALU.add, [("ps2b", gi % 2)], [("ps1b", gi % 2)])
                else:
                    assert n == 2
                    tt("dve", s1[:], ptb[:, 0, :], ptb[:, 1, :], ALU.add, [("pt", gi % NPT)], [("ps1b", gi % 2)])
                for ci, c in enumerate(grp):
                    first = gi == 0 and ci == 0
                    lastc = gi == ngr - 1 and ci == len(grp) - 1
                    mmgroup(psum[:, po, :], [(Vres[:, c, kv * 128:(kv + 1) * 128], ptb[:, ci, :])],
                            r=[("V", c), ("pt", gi % NPT)], w=[("ps", po)], start=first, stop=lastc)
                if OSUM:
                    mmgroup(psum[:, psm, :], [(onesb[:], s1[:])],
                            r=["onesb", ("ps1b", gi % 2)], w=[("ps", psm)], start=(gi == 0), stop=(gi == ngr - 1))
                else:
                    for ci, c in enumerate(grp):
                        mmgroup(psum[:, psm, :], [(onesb[:], ptb[:, ci, :])], r=["onesb", ("pt", gi % NPT)], w=[("ps", psm)],
                                start=(gi == 0 and ci == 0), stop=(gi == ngr - 1 and ci == n - 1))
            emit_S(0)
            if ngr > 1:
                emit_S(1)
            for gi in range(ngr):
                if gi + 2 < ngr:
                    emit_S(gi + 2)
                emit_PV(gi)
            rinv, kri = T32()
            S.add("dve", lambda e, rinv=rinv, psm=psm: e.reciprocal(out=rinv[:], in_=psum[:, psm, :]), r=[("ps", psm)], w=[kri])
            tt("dve", pa[:, 8 + j, :], psum[:, po, :], rinv[:], ALU.mult, [("ps", po), kri], [("pa", 8 + j)])
        cur_pool[0] = "all"
        wdsc, kdsc = wg(l, G_DSC)
        for half in range(2):
            wb_, kb_ = wg(l, G_SCB + half)
            wb8 = k8(wb_)
            for oc in range(4):
                j = half * 4 + oc
                pb = PS()
                mmgroup(psum[:, pb, :], [(wb8[:, k, oc * 128:(oc + 1) * 128], hbuf[:, k, :]) for k in range(KC)], r=[kb_, hkey], w=[("ps", pb)])
                bg, kbg = T32()
                cp("act", bg[:], psum[:, pb, :], [("ps", pb)], [kbg])
                pc = PS()
                mmgroup(psum[:, pc, :], [(wdsc[:, (j * 3 + k) * 128:(j * 3 + k + 1) * 128], zwin[:, j, k:k + T]) for k in range(3)],
                        r=[kdsc, "zwin"], w=[("ps", pc)])
                tt("dve", pa[:, 16 + j, :], psum[:, pc, :], bg[:], ALU.mult, [("ps", pc), kbg], [("pa", 16 + j)])
        for j in range(KC):
            wd, kd = wg(l, G_DCF + j)
            pc = PS()
            mmgroup(psum[:, pc, :], [(wd[:, k * 128:(k + 1) * 128], uwin[:, j, k:k + T]) for k in range(31)],
                    r=[kd, "uwin"], w=[("ps", pc)])
            act(big32[:, j, :], psum[:, pc, :], AF.Identity, r=[("ps", pc), ("lnv", l)], w=[("big32", j)], bias=lnv[l][:, 2, j:j + 1])
            cp("pool", mh[:, j, :], big32[:, j, :], [("big32", j)], [("mh", j)])
            act(sq[:, j, :], big32[:, j, :], AF.Square, r=[("big32", j)], w=[("sq", j)])
        pm = PS()
        pq = PS()
        mmgroup(psum[:, pm, :], [(onesb[:], mh[:, k, :]) for k in range(KC)], r=["onesb"] + [("mh", k) for k in range(KC)], w=[("ps", pm)])
        mmgroup(psum[:, pq, :], [(onesb[:], sq[:, k, :]) for k in range(KC)], r=["onesb"] + [("sq", k) for k in range(KC)], w=[("ps", pq)])
        mean, kmean = T32()
        ts("dve", mean[:], psum[:, pm, :], 1.0 / D, None, ALU.mult, None, [("ps", pm)], [kmean])
        msq, kmsq = T32()
        tt("dve", msq[:], mean[:], mean[:], ALU.mult, [kmean], [kmsq])
        var, kvar = T32()
        stt("dve", var[:], psum[:, pq, :], 1.0 / D, msq[:], ALU.mult, ALU.subtract, [("ps", pq), kmsq], [kvar])
        rstd, kr = rsqrt_from(var[:], [kvar, "small"], 1.0, LN_EPS)
        tt("dve", big32[:], big32[:], mean[:].unsqueeze(1).to_broadcast([128, KC, T]), ALU.subtract, b32k + [kmean], b32k)
        tt("dve", big32[:], big32[:], rstd[:].unsqueeze(1).to_broadcast([128, KC, T]), ALU.mult, b32k + [kr], b32k)
        for j in range(KC):
            act(pa[:, 24 + j, :], big32[:, j, :], AF.Silu, r=[("big32", j), ("lnv", l)], w=[("pa", 24 + j)],
                scale=lnv[l][:, 0, j:j + 1], bias=lnv[l][:, 1, j:j + 1])
        if nxt is not None:
            p2_loads(l, nxt[0], nxt[1], hi + 1)
        for half in range(2):
            for oc in range(4):
                dc = half * 4 + oc
                if oc == 0:
                    wao, kao = wg(l, G_AO + half)
                    wga, kga = wg(l, G_GA + half)
                pya = PS()
                pga = PS()
                mmgroup(psum[:, pya, :], [(k8(wao)[:, k, oc * 128:(oc + 1) * 128], pa[:, 8 + k, :]) for k in range(KC)],
                        r=[kao] + [("pa", 8 + k) for k in range(KC)], w=[("ps", pya)])
                mmgroup(psum[:, pga, :], [(k8(wga)[:, k, oc * 128:(oc + 1) * 128], hbuf[:, k, :]) for k in range(KC)],
                        r=[kga, hkey], w=[("ps", pga)])
                ga, kga_t = T32()
                act(ga[:], psum[:, pga, :], AF.Sigmoid, r=[("ps", pga)], w=[kga_t])
                tt("dve", big32[:, dc, :], psum[:, pya, :], ga[:], ALU.mult, [("ps", pya), kga_t], [("big32", dc)])
        for half in range(2):
            for oc in range(4):
                dc = half * 4 + oc
                if oc == 0:
                    wso, kso = wg(l, G_SO + half)
                    wgb, kgb = wg(l, G_GB + half)
                pys = PS()
                pgb = PS()
                mmgroup(psum[:, pys, :], [(k8(wso)[:, k, oc * 128:(oc + 1) * 128], pa[:, 16 + k, :]) for k in range(KC)],
                        r=[kso] + [("pa", 16 + k) for k in range(KC)], w=[("ps", pys)])
                mmgroup(psum[:, pgb, :], [(k8(wgb)[:, k, oc * 128:(oc + 1) * 128], hbuf[:, k, :]) for k in range(KC)],
                        r=[kgb, hkey], w=[("ps", pgb)])
                gb, kgb_t = T32()
                act(gb[:], psum[:, pgb, :], AF.Sigmoid, r=[("ps", pgb)], w=[kgb_t])
                mb, kmb = T32()
                tt("dve", mb[:], psum[:, pys, :], gb[:], ALU.mult, [("ps", pys), kgb_t], [kmb])
                tt("pool", big32[:, dc, :], big32[:, dc, :], mb[:], ALU.add, [("big32", dc), kmb], [("big32", dc)])
        for half in range(2):
            for oc in range(4):
                dc = half * 4 + oc
                if oc == 0:
                    wco, kco = wg(l, G_CO + half)
                    wgc, kgc = wg(l, G_GC + half)
                pyc = PS()
                pgc = PS()
                mmgroup(psum[:, pyc, :], [(k8(wco)[:, k, oc * 128:(oc + 1) * 128], pa[:, 24 + k, :]) for k in range(KC)],
                        r=[kco] + [("pa", 24 + k) for k in range(KC)], w=[("ps", pyc)])
                mmgroup(psum[:, pgc, :], [(k8(wgc)[:, k, oc * 128:(oc + 1) * 128], hbuf[:, k, :]) for k in range(KC)],
                        r=[kgc, hkey], w=[("ps", pgc)])
                gc, kgc_t = T32()
                act(gc[:], psum[:, pgc, :], AF.Sigmoid, r=[("ps", pgc)], w=[kgc_t])
                mc, kmc = T32()
                stt("dve", mc[:], psum[:, pyc, :], lnv[l][:, 3, dc:dc + 1], gc[:], ALU.add, ALU.mult,
                    [("ps", pyc), kgc_t, ("lnv", l)], [kmc])
                tt("dve", mh[:, dc, :], big32[:, dc, :], mc[:], ALU.add, [("big32", dc), kmc], [("mh", dc)])
        for half in range(2):
            wo, kwo = wg(l, G_WO + half)
            for oc in range(4):
                dc = half * 4 + oc
                po_ = PS()
                mmgroup(psum[:, po_, :], [(k8(wo)[:, k, oc * 128:(oc + 1) * 128], mh[:, k, :]) for k in range(KC)],
                        r=[kwo] + [("mh", k) for k in range(KC)], w=[("ps", po_)])
                stt("dve", xT[:, dc, :], psum[:, po_, :], modT[l][:, 2, dc, col:col + 1], xT[:, dc, :], ALU.mult, ALU.add,
                    [("ps", po_), ("modT", l), xkey], [xkey])
        act(sq[:].rearrange("p k t -> p (k t)"), xT[:].rearrange("p k t -> p (k t)"), AF.Square, r=[xkey], w=sqk)
        pss = PS()
        mmgroup(psum[:, pss, :], [(onesb[:], sq[:, k, :]) for k in range(KC)], r=["onesb"] + [("sq", k) for k in range(KC)], w=[("ps", pss)])
        rstd2, kr2 = rsqrt_from(psum[:, pss, :], [("ps", pss), "small"], 1.0 / D, NORM_EPS)
        tt("dve", big32[:], xT[:], rstd2[:].unsqueeze(1).to_broadcast([128, KC, T]), ALU.mult, [xkey, kr2], b32k)
        for k in range(KC):
            ts("dve", mh[:, k, :], big32[:, k, :], avec[l][:, 1, col, k:k + 1], modT[l][:, 3, k, col:col + 1],
               ALU.mult, ALU.add, [("big32", k), ("avec", l), ("modT", l)], [("mh", k)])
        mhk = [("mh", k) for k in range(KC)]
        for g in range(8):
            w1, kw1 = wg(l, G_M1 + g)
            w18 = k8(w1)
            for oc in range(4):
                fc = g * 4 + oc
                ph = PS()
                mmgroup(psum[:, ph, :], [(w18[:, k, oc * 128:(oc + 1) * 128], mh[:, k, :]) for k in range(KC)], r=[kw1] + mhk, w=[("ps", ph)])
                rl, krl = TB16()
                act(rl[:], psum[:, ph, :], AF.Relu, r=[("ps", ph)], w=[krl])
                tt("pool", pa[:, fc, :], rl[:], rl[:], ALU.mult, [krl], [("pa", fc)])
        pak = [("pa", i) for i in range(32)]
        for dc in range(KC):
            w2, kw2 = wg(l, G_M2 + dc)
            w2v = w2.rearrange("p (k n) -> p k n", k=FC)
            po_ = PS()
            mmgroup(psum[:, po_, :], [(w2v[:, f, :], pa[:, f, :]) for f in range(FC)], r=[kw2] + pak, w=[("ps", po_)])
            stt("dve", xT[:, dc, :], psum[:, po_, :], modT[l][:, 5, dc, col:col + 1], xT[:, dc, :], ALU.mult, ALU.add,
                [("ps", po_), ("modT", l), xkey], [xkey])
        if not (last and lat):
            dma("sp", xtd[ti], xT[:].rearrange("p k t -> p (k t)"), [xkey], [(nm + "xTd", ti)], "xst")
        else:
            for b in range(CPT):
                bankA = psum[:, 12:14, :].rearrange("p a b -> p (a b)")
                bankB = psum[:, 14:16, :].rearrange("p a b -> p (a b)")
                S.add("pe", lambda e, b=b: [e.transpose((bankA if k < 4 else bankB)[:, (k % 4) * 128:(k % 4 + 1) * 128], xT[:, k, b * 128:(b + 1) * 128], ident[:]) for k in range(KC)][-1],
                      r=[xkey, "ident"], w=[("ps", i) for i in (12, 13, 14, 15)])
                full = psum[:, 12:16, :].rearrange("p a b -> p (a b)")
                junk = big32[:].rearrange("p k t -> p (k t)")[:, 0:D]
                ssb, kss = T32()
                S.add("dve", lambda e, ssb=ssb: e.memset(ssb[:, 0:1], 0.0), w=[kss])
                act(junk, full, AF.Square, r=[("ps", i) for i in (12, 13, 14, 15)], w=b32k[0:4] + [kss], accum_out=ssb[:, 0:1])
                l1, kl1 = T32()
                act(l1[:, 0:1], ssb[:, 0:1], AF.Sqrt, r=[kss, "small"], w=[kl1], scale=1.0 / D, bias=eps_a[:])
                rs, krs = T32()
                S.add("dve", lambda e, rs=rs, l1=l1: e.reciprocal(out=rs[:, 0:1], in_=l1[:, 0:1]), r=[kl1], w=[krs])
                stt("dve", xtok[:, b, :], full, rs[:, 0:1], gfin[:], ALU.mult, ALU.mult,
                    [("ps", i) for i in (12, 13, 14, 15)] + [krs, "gfin"], [("xtok", b)])
                dma("sp", out_d[t0 + b * 128:t0 + (b + 1) * 128, :], xtok[:, b, :], [("xtok", b)], [("out", ti, b)], "outst")
        if nxt is not None:
            p2_xload(nxt[0], nxt[1])

    for l in range(n_layers):
        if debug_stage is not None and debug_stage < 2:
            break
        last = l == n_layers - 1
        phase1(l, "ctx", 0)
        if debug_stage is not None and 2 <= debug_stage < 3:
            break
        for ti in range(NT):
            phase1(l, "lat", ti)
        if debug_stage == 3:
            break
        cur_pool[0] = "all"
        seq = [("lat", ti) for ti in range(NT)] + ([] if last else [("ctx", 0)])
        p2_loads(l, seq[0][0], seq[0][1], 0)
        p2_xload(seq[0][0], seq[0][1])
        for i, (kd, ti) in enumerate(seq):
            phase2(l, kd, ti, i, last, seq[i + 1] if i + 1 < len(seq) else None)
        cur_pool[0] = "all"

    S.finalize()

    if os.environ.get("KDUMP"):
        for e in Sched.ENGS:
            print("ENGINE", e, len(S.q[e]))
            for i, op in enumerate(S.q[e][-int(os.environ["KDUMP"]):]):
                print("  ", i, "dma" if op.is_dma else "", op.dkey, "sig" if op.sig else "", op.cnt, op.waits)
    if os.environ.get("KSIM"):
        semv = {}
        pos = {e: 0 for e in Sched.ENGS}
        progress = True
        while progress:
            progress = False
            for e in Sched.ENGS:
                while pos[e] < len(S.q[e]):
                    op = S.q[e][pos[e]]
                    if all(semv.get(k, 0) >= v for k, v in op.waits):
                        if op.is_dma:
                            semv[("d", op.dkey)] = semv.get(("d", op.dkey), 0) + 16
                        elif op.sig:
                            k = ("e", op.eng, (op.cnt - 1) // EPOCH)
                            semv[k] = semv.get(k, 0) + 1
                        pos[e] += 1
                        progress = True
                    else:
                        break
        for e in Sched.ENGS:
            print("SIM", e, pos[e], "/", len(S.q[e]))
            if pos[e] < len(S.q[e]):
                op = S.q[e][pos[e]]
                print("   stuck on waits", [(k, v, semv.get(k, 0)) for k, v in op.waits])
    sems = {}

    def getsem(key):
        s = sems.get(key)
        if s is None:
            s = es.enter_context(nc.semaphore("s%d" % len(sems)))
            sems[key] = s
        return s

    for e in Sched.ENGS:
        for op in S.q[e]:
            for key, val in op.waits:
                getsem(key)
            if op.is_dma:
                getsem(("d", op.dkey))
            elif op.sig:
                getsem(("e", op.eng, (op.cnt - 1) // EPOCH))


    if os.environ.get("KDUMP"):
        print("NSEMS", len(sems), [ (k, getattr(v, "num", None)) for k, v in sems.items()])

    def emit(eng_obj, ops, final_keys=(), throttle=0):
        issued = {}
        for op in ops:
            for key, val in op.waits:
                eng_obj.wait_ge(sems[key], val)
            if throttle and op.is_dma:
                n = issued.get(op.dkey, 0)
                if n > 0 and n % throttle == 0:
                    eng_obj.wait_ge(sems[("d", op.dkey)], 16 * n)
                issued[op.dkey] = n + 1
            ins = op.fn(eng_obj)
            if op.is_dma:
                ins.then_inc(sems[("d", op.dkey)], 16)
            elif op.sig:
                ins.then_inc(sems[("e", op.eng, (op.cnt - 1) // EPOCH)], 1)
        for k in final_keys:
            if k in S.dmacnt:
                eng_obj.wait_ge(sems[("d", k)], 16 * S.dmacnt[k])

    with nc.Block() as block:
        @block.sync
        def _(e):
            emit(e, S.q["sp"])

        @block.tensor
        def _(e):
            emit(e, S.q["pe"])

        @block.scalar
        def _(e):
            emit(e, S.q["act"])

        @block.vector
        def _(e):
            emit(e, S.q["dve"])

        @block.gpsimd
        def _(e):
            emit(e, S.q["pool"], final_keys=list(S.dmacnt.keys()), throttle=int(os.environ.get("THR", "2")))
    es.close()
    return nc


def rope_tables(S_len):
    GRID_W = 64
    t = np.arange(S_len)
    pos = np.stack([t // GRID_W, t % GRID_W], axis=0).astype(np.float32)
    inv_freq = (10000.0 ** (-np.arange(32, dtype=np.float32) * 2.0 / 64.0)).astype(np.float32)
    ang = pos[:, None, :] * inv_freq[None, :, None]
    cos = np.cos(ang).astype(np.float32)
    sin = np.sin(ang).astype(np.float32)
    cosT = np.zeros((128, S_len), np.float32)
    sinT = np.zeros((128, S_len), np.float32)
    for ax in range(2):
        for half in range(2):
            p0 = ax * 64 + half * 32
            cosT[p0:p0 + 32] = cos[ax]
            sinT[p0:p0 + 32] = sin[ax] * (-1.0 if half == 0 else 1.0)
    perm = np.zeros((128, 128), np.float32)
    for m in range(128):
        partner = m + 32 if (m % 64) < 32 else m - 32
        perm[partner, m] = 1.0
    return cosT, sinT, perm


def make_in_maps(inp, S_len, nb):
    f = lambda a: np.ascontiguousarray(np.asarray(a, dtype=np.float32))
    cosT, sinT, perm = rope_tables(S_len)
    pv = np.zeros((L, NPV, D), np.float32)
    pv[:, R_G1] = inp["g_norm1"]
    pv[:, R_G2] = inp["g_norm2"]
    pv[:, R_BCFC] = inp["b_cf_conv"]
    pv[:, R_GLN] = inp["g_cf_ln"]
    pv[:, R_BLN] = inp["b_cf_ln"]
    pv[:, R_BCFO] = inp["b_cf_out"]
    pv[:, R_BMOD:R_BMOD + 6] = np.asarray(inp["b_mod"]).reshape(L, 6, D)
    pv[:, R_WSC:R_WSC + 3] = inp["w_sc_conv"]
    pv[:, R_WCF:R_WCF + 31] = inp["w_cf_conv"]
    qk = np.stack([np.asarray(inp["q_gain"]), np.asarray(inp["k_gain"])], axis=1).astype(np.float32)
    shared = {
        "w_mod": f(inp["w_mod"]), "w_in": f(inp["w_in"]), "w_attn_out": f(inp["w_attn_out"]),
        "w_sc_out": f(inp["w_sc_out"]), "w_cf_out": f(inp["w_cf_out"]), "w_o": f(inp["w_o"]),
        "w_mlp_in": f(inp["w_mlp_in"]), "w_mlp_out": f(inp["w_mlp_out"]), "pv": pv, "qk": f(qk),
        "g_final": f(inp["g_final"]), "rope_cos": cosT, "rope_sin": sinT,
        "ident": np.eye(128, dtype=np.float32), "perm": perm,
    }
    x = np.asarray(inp["x"]); c = np.asarray(inp["c"]); ctx = np.asarray(inp["ctx"]); c_ctx = np.asarray(inp["c_ctx"])
    maps = []
    for b in range(nb):
        m = dict(shared)
        m["x"] = f(x[b])
        m["ctx"] = f(ctx[b])
        m["cc"] = f(np.stack([c[b], c_ctx], axis=0))
        maps.append(m)
    return maps


def kernel(**inputs):
    x = np.asarray(inputs["x"])
    B, S_len, _ = x.shape
    nc = build(S_len)
    in_maps = make_in_maps(inputs, S_len, B)
    res = run_bass_kernel_spmd(nc, in_maps, core_ids=list(range(B)))
    return np.stack([np.asarray(r["out"]) for r in res.results], axis=0).astype(np.float32)
```

```python
import os
import numpy as np
from contextlib import ExitStack
import concourse.bass as bass
import concourse.mybir as mybir
from concourse.bass_utils import run_bass_kernel_spmd

F32 = mybir.dt.float32
BF16 = mybir.dt.bfloat16
AF = mybir.ActivationFunctionType
ALU = mybir.AluOpType

D = 1024
KC = 8
HD = 128
NQH = 8
NKVH = 2
CTXL = 256
DFF = 4096
FC = 32
DIN = 9728
L = 2
TT = 256
NSLOT = 4
NG = 52
NORM_EPS = 1e-6
LN_EPS = 1e-5
EPOCH = 50000
R_G1, R_G2, R_BCFC, R_GLN, R_BLN, R_BCFO, R_BMOD, R_WSC, R_WCF, NPV = 0, 1, 2, 3, 4, 5, 6, 12, 15, 46
G_Q, G_KV, G_SCB, G_SCC, G_SCH, G_CFA, G_CFG, G_GA, G_GB, G_GC = 0, 2, 3, 5, 7, 9, 11, 13, 15, 17
G_AO, G_SO, G_CO, G_WO, G_M1, G_M2, G_DCF, G_DSC = 19, 21, 23, 25, 27, 35, 43, 51


class Op:
    __slots__ = ("eng", "fn", "deps", "sig", "cnt", "dkey", "dval", "is_dma", "waits")


class Sched:
    ENGS = ("pe", "act", "dve", "pool", "sp")

    def __init__(self):
        self.q = {e: [] for e in self.ENGS}
        self.lastw = {}
        self.rd = {}
        self.dmacnt = {}

    def add(self, eng, fn, r=(), w=(), dma=None):
        r = [("ps", k[1] // 2) if (isinstance(k, tuple) and k[0] == "ps") else k for k in r]
        w = [("ps", k[1] // 2) if (isinstance(k, tuple) and k[0] == "ps") else k for k in w]
        op = Op()
        op.eng = eng
        op.fn = fn
        op.is_dma = dma is not None
        op.sig = False
        op.cnt = 0
        op.dkey = dma
        op.dval = 0
        if dma is not None:
            n = self.dmacnt.get(dma, 0) + 1
            self.dmacnt[dma] = n
            op.dval = 16 * n
        deps = {}
        lastw = self.lastw
        rd = self.rd

        def consider(d, raw):
            if d is op:
                return
            if (not d.is_dma) and (not op.is_dma) and d.eng == eng and not raw and eng == "pe":
                return
            deps[id(d)] = d

        for k in r:
            d = lastw.get(k)
            if d is not None:
                consider(d, True)
        for k in w:
            d = lastw.get(k)
            if d is not None:
                consider(d, False)
            rr = rd.get(k)
            if rr:
                for e2, o in rr.items():
                    if e2 == "dma":
                        for o2 in o:
                            consider(o2, False)
                    else:
                        consider(o, False)
        for k in r:
            rr = rd.get(k)
            if rr is None:
                rr = rd[k] = {}
            if op.is_dma:
                rr.setdefault("dma", []).append(op)
            else:
                rr[eng] = op
        for k in w:
            lastw[k] = op
            rd[k] = {}
        op.deps = []
        op.waits = {}
        for d in deps.values():
            if d.is_dma:
                key = ("d", d.dkey)
                val = 16 * self.dmacnt[d.dkey]
                if op.dkey == d.dkey:
                    val -= 16
                if op.waits.get(key, 0) < val:
                    op.waits[key] = val
            else:
                d.sig = True
                op.deps.append(d)
        self.q[eng].append(op)
        return op

    def finalize(self):
        self.nsig = {}
        for e in self.ENGS:
            c = 0
            for op in self.q[e]:
                if op.sig and not op.is_dma:
                    c += 1
                    op.cnt = c
            self.nsig[e] = c
        for e in self.ENGS:
            seen = {}
            for op in self.q[e]:
                waits = op.waits
                for d in op.deps:
                    ep = (d.cnt - 1) // EPOCH
                    key = ("e", d.eng, ep)
                    val = d.cnt - ep * EPOCH
                    if waits.get(key, 0) < val:
                        waits[key] = val
                op.waits = []
                for key, val in waits.items():
                    if seen.get(key, 0) >= val:
                        continue
                    seen[key] = val
                    op.waits.append((key, val))
                op.deps = None


def build(S_len, n_layers=L, debug_stage=None):
    nc = bass.Bass("TRN2", target_bir_lowering=False)
    T = TT
    assert T == CTXL and S_len % T == 0
    NT = S_len // T
    NKEY = CTXL + S_len
    NCH = NKEY // 128
    CPT = T // 128

    def din(name, shape, dt=F32):
        return nc.dram_tensor(name, list(shape), dt, kind="ExternalInput").ap()

    def dscr(name, shape, dt):
        return nc.dram_tensor(name, list(shape), dt, kind="Internal").ap()

    x_d = din("x", [S_len, D])
    ctx_d = din("ctx", [CTXL, D])
    cc_d = din("cc", [2, D])
    wmod_d = din("w_mod", [L, D, 6 * D])
    win_d = din("w_in", [L, D, DIN])
    wsq_d = [din(n, [L, D, D]) for n in ("w_attn_out", "w_sc_out", "w_cf_out", "w_o")]
    wm1_d = din("w_mlp_in", [L, D, DFF])
    wm2_d = din("w_mlp_out", [L, DFF, D])
    pv_d = din("pv", [L, NPV, D])
    qk_d = din("qk", [L, 2, HD])
    gfin_d = din("g_final", [D])
    cos_d = din("rope_cos", [128, S_len])
    sin_d = din("rope_sin", [128, S_len])
    ident_d = din("ident", [128, 128])
    perm_d = din("perm", [128, 128])
    out_d = nc.dram_tensor("out", [S_len, D], F32, kind="ExternalOutput").ap()

    xTd = dscr("xTd", [NT, 128, KC * T], F32)
    hTd = dscr("hTd", [NT, 128, KC * T], BF16)
    uTd = dscr("uTd", [KC, 128, S_len + 30], BF16)
    zTd = dscr("zTd", [KC, 128, S_len + 2], BF16)
    cxTd = dscr("cxTd", [2, 128, KC * T], F32)
    chTd = dscr("chTd", [2, 128, KC * T], BF16)
    cuTd = dscr("cuTd", [KC, 128, CTXL + 30], BF16)
    czTd = dscr("czTd", [KC, 128, CTXL + 2], BF16)
    wbf = dscr("wbf", [L, NG, 128, 4096], BF16)

    S = Sched()
    es = ExitStack()

    def sb(name, shape, dt):
        return es.enter_context(nc.sbuf_tensor("sb_" + name, list(shape), dt))

    Kres = sb("Kres", [128, NKVH, NKEY], BF16)
    Vres = sb("Vres", [128, NCH, NKVH * HD], BF16)
    wslot = [sb(f"ws{i}", [128, 4096], BF16) for i in range(NSLOT)]
    xT = sb("xT", [128, KC, T], F32)
    hT = [sb(f"hT{i}", [128, KC, T], BF16) for i in range(2)]
    uwin = sb("uwin", [128, KC, T + 30], BF16)
    zwin = sb("zwin", [128, KC, T + 2], BF16)
    cs = [sb(f"cs{i}", [128, 2, T], F32) for i in range(2)]
    pa = sb("pa", [128, 32, T], BF16)
    big32 = sb("big32", [128, KC, T], F32)
    mh = sb("mh", [128, KC, T], BF16)
    NPT = 4
    pt = [sb(f"pt{i}", [128, 4, T], BF16) for i in range(NPT)]
    xtok = sb("xtok", [128, 2, D], F32)
    sq = sb("sq", [128, KC, T], BF16)
    NTMP = 7
    tmp = [sb(f"tmp{i}", [128, T], F32) for i in range(NTMP)]
    tmpb = [sb(f"tmpb{i}", [128, T], BF16) for i in range(4)]
    qnb = [sb(f"qnb{i}", [128, T], BF16) for i in range(4)]
    ps2b = [sb(f"ps2b{i}", [128, 2, T], BF16) for i in range(2)]
    ps1b = [sb(f"ps1b{i}", [128, T], BF16) for i in range(2)]
    ident = sb("ident", [128, 128], F32)
    identb = sb("identb", [128, 128], BF16)
    permf = sb("permf", [128, 128], F32)
    permb = sb("permb", [128, 128], BF16)
    onesb = sb("onesb", [128, 128], BF16)
    zeros = sb("zeros", [128, KC, 16], BF16)
    pvrow = sb("pvrow", [128, D], F32)
    ccrow = sb("ccrow", [128, D], F32)
    qkrow = sb("qkrow", [128, HD], F32)
    gfin = sb("gfin", [128, D], F32)
    scT = sb("scT", [128, KC, 2], F32)
    pvT = [sb(f"pvT{l}", [128, KC, NPV], F32) for l in range(L)]
    qkT = [sb(f"qkT{l}", [128, 2], F32) for l in range(L)]
    modT = [sb(f"modT{l}", [128, 6, KC, 2], F32) for l in range(L)]
    avec = [sb(f"avec{l}", [128, 2, 2, KC], F32) for l in range(L)]
    lnv = [sb(f"lnv{l}", [128, 4, KC], F32) for l in range(L)]
    small = sb("small", [128, 8], F32)
    eps_a = sb("eps_a", [128, 1], F32)
    eps_b = sb("eps_b", [128, 1], F32)
    eps_c = sb("eps_c", [128, 1], F32)
    psum = es.enter_context(nc.psum_tensor("psum", [128, 16, 256], F32))

    if os.environ.get("KDUMP"):
        print("SBUF remaining", nc.sbuf_bytes_remaining)
    tmp_i = [0]

    def T32():
        i = tmp_i[0] % NTMP
        tmp_i[0] += 1
        return tmp[i], ("tmp", i)

    tmpb_i = [0]

    def TB16():
        i = tmpb_i[0] % 4
        tmpb_i[0] += 1
        return tmpb[i], ("tmpb", i)

    ps_rot = {"gen": [12, 14], "all": [0, 2, 4, 6, 8, 10, 12, 14]}
    ps_idx = {"gen": 0, "all": 0}

    cur_pool = ["all"]

    def PS(pool=None):
        pool = pool or cur_pool[0]
        lst = ps_rot[pool]
        i = lst[ps_idx[pool] % len(lst)]
        ps_idx[pool] += 1
        return i

    wcount = [0]

    def wload(src_ap, rkeys, f32=False, ncol=4096):
        s = wcount[0] % NSLOT
        wcount[0] += 1
        dst = wslot[s][:, 0:ncol]
        if f32:
            dst = dst.bitcast(F32)
        S.add("sp", lambda e, dst=dst, src=src_ap: e.dma_start(out=dst, in_=src),
              r=rkeys, w=[("ws", s)], dma=("ws", s))
        return wslot[s], ("ws", s)

    def wg(l, g):
        rk = [("wbf", l, g)] if not (G_M2 <= g < G_M2 + 8) else [("wbf", l, g, q4) for q4 in range(4)]
        ncol = 3072 if g == G_DSC else (3968 if g >= G_DCF else 4096)
        slot, key = wload(wbf[l, g, :, 0:ncol], rk, ncol=ncol)
        return slot[:], key

    def k8(slot_ap, n=512):
        return slot_ap.rearrange("p (k n) -> p k n", k=KC)

    def mmgroup(ps_ap, pairs, r, w, start=True, stop=True):
        n = len(pairs)

        def fn(e):
            ins = None
            for i, (a, b) in enumerate(pairs):
                ins = e.matmul(ps_ap, a, b, start=(start and i == 0), stop=(stop and i == n - 1))
            return ins
        S.add("pe", fn, r=r, w=w)

    def act(out, in_, func, r, w, bias=None, scale=None, accum_out=None):
        kw = {}
        if bias is not None:
            kw["bias"] = bias
        if scale is not None:
            kw["scale"] = scale
        if accum_out is not None:
            kw["accum_out"] = accum_out
        S.add("act", lambda e: e.activation(out=out, in_=in_, func=func, **kw), r=r, w=w)

    def tt(eng, out, in0, in1, op, r, w):
        S.add(eng, lambda e: e.tensor_tensor(out=out, in0=in0, in1=in1, op=op), r=r, w=w)

    def ts(eng, out, in0, s1, s2, op0, op1, r, w):
        if s2 is None:
            S.add(eng, lambda e: e.tensor_scalar(out=out, in0=in0, scalar1=s1, scalar2=None, op0=op0), r=r, w=w)
        else:
            S.add(eng, lambda e: e.tensor_scalar(out=out, in0=in0, scalar1=s1, scalar2=s2, op0=op0, op1=op1), r=r, w=w)

    def stt(eng, out, in0, scalar, in1, op0, op1, r, w):
        S.add(eng, lambda e: e.scalar_tensor_tensor(out=out, in0=in0, scalar=scalar, in1=in1, op0=op0, op1=op1), r=r, w=w)

    def cp(eng, out, in_, r, w):
        if eng == "act":
            S.add("act", lambda e: e.activation(out=out, in_=in_, func=AF.Copy), r=r, w=w)
        else:
            S.add(eng, lambda e: e.tensor_copy(out=out, in_=in_), r=r, w=w)

    def dma(q, out, in_, r, w, key, slow=False):
        if slow:
            S.add(q, lambda e: e.dma_start(out=out, in_=in_, allow_slow_non_contiguous=True), r=r, w=w, dma=key)
        else:
            S.add(q, lambda e: e.dma_start(out=out, in_=in_), r=r, w=w, dma=key)

    def rsqrt_from(ps_or_sb, rkeys, scale, eps):
        t1, k1 = T32()
        act(t1[:], ps_or_sb, AF.Sqrt, r=rkeys, w=[k1], scale=scale, bias=eps_ap(eps))
        t2, k2 = T32()
        S.add("dve", lambda e, t1=t1, t2=t2: e.reciprocal(out=t2[:], in_=t1[:]), r=[k1], w=[k2])
        return t2, k2

    eps_tiles = {}

    def eps_ap(v):
        return eps_tiles[v]

    S.add("dve", lambda e: e.memset(onesb[:], 1.0), w=["onesb"])
    S.add("dve", lambda e: e.memset(zeros[:], 0.0), w=["zeros"])
    S.add("dve", lambda e: e.memset(eps_a[:], NORM_EPS), w=["small"])
    S.add("dve", lambda e: e.memset(eps_b[:], LN_EPS), w=["small"])
    S.add("dve", lambda e: e.memset(eps_c[:], 128.0 * NORM_EPS), w=["small"])
    eps_tiles[NORM_EPS] = eps_a[:]
    eps_tiles[LN_EPS] = eps_b[:]
    dma("act", ident[:], ident_d, [], ["ident"], "c_ident")
    dma("act", permf[:], perm_d, [], ["permf"], "c_perm")
    dma("act", ccrow[0:2, :], cc_d, [], ["ccrow"], "c_cc")
    dma("act", gfin[:], gfin_d.partition_broadcast(128), [], ["gfin"], "c_gfin")
    cp("dve", identb[:], ident[:], ["ident"], ["identb"])
    cp("dve", permb[:], permf[:], ["permf"], ["permb"])
    for (td, n, padw, nm) in ((uTd, S_len, 15, "luTd"), (zTd, S_len, 1, "lzTd"), (cuTd, CTXL, 15, "cuTd"), (czTd, CTXL, 1, "czTd")):
        dma("pool", td[:, :, 0:padw].rearrange("k p t -> p k t"), zeros[:, :, 0:padw], ["zeros"], [(nm, "padl")], "c_pad", slow=True)
        dma("pool", td[:, :, padw + n:padw + n + padw].rearrange("k p t -> p k t"), zeros[:, :, 0:padw], ["zeros"], [(nm, "padr")], "c_pad", slow=True)

    def cast_group(l, g):
        dst = wbf[l, g]
        if g < 19:
            src = win_d[l, :, g * 512:(g + 1) * 512].rearrange("(k p) n -> p k n", p=128)
            dv = dst.rearrange("p (k n) -> p k n", k=KC)
        elif g < 27:
            wi, h = (g - 19) // 2, (g - 19) % 2
            src = wsq_d[wi][l, :, h * 512:(h + 1) * 512].rearrange("(k p) n -> p k n", p=128)
            dv = dst.rearrange("p (k n) -> p k n", k=KC)
        elif g < 35:
            src = wm1_d[l, :, (g - 27) * 512:(g - 26) * 512].rearrange("(k p) n -> p k n", p=128)
            dv = dst.rearrange("p (k n) -> p k n", k=KC)
        else:
            dc = g - 35
            src = wm2_d[l, :, dc * 128:(dc + 1) * 128].rearrange("(k p) n -> p k n", p=128)
            dv = dst.rearrange("p (k n) -> p k n", k=FC)
            for q4 in range(4):
                dma("pool", dv[:, q4 * 8:(q4 + 1) * 8, :], src[:, q4 * 8:(q4 + 1) * 8, :], [], [("wbf", l, g, q4)], ("cast", l, 2))
            return
        dma("pool", dv, src, [], [("wbf", l, g)], ("cast", l, 0 if g in (2, 5, 6, 7, 8, 9, 10, 11, 12) else 1))

    cast_order = [2, 5, 6, 7, 8, 9, 10, 11, 12, 0, 1, 3, 4] + list(range(13, 43))
    for l in range(n_layers):
        for g in cast_order:
            cast_group(l, g)

    act(ccrow[0:2, :], ccrow[0:2, :], AF.Silu, r=["ccrow"], w=["ccrow"])
    S.add("pe", lambda e: [e.transpose(psum[:, 0, 2 * k:2 * k + 2], ccrow[0:2, k * 128:(k + 1) * 128], ident[0:2, 0:2]) for k in range(KC)][-1],
          r=["ccrow", "ident"], w=[("ps", 0)])
    cp("dve", scT[:].rearrange("p k c -> p (k c)"), psum[:, 0, 0:16], [("ps", 0)], ["scT"])

    def prep(l):
        dma("act", pvrow[0:NPV, :], pv_d[l], [], ["pvrow"], "c_pv")
        dma("act", qkrow[0:2, :], qk_d[l], [], ["qkrow"], "c_qk")
        bank = psum[:, 0:2, :].rearrange("p a b -> p (a b)")
        S.add("pe", lambda e: [e.transpose(bank[:, k * NPV:(k + 1) * NPV], pvrow[0:NPV, k * 128:(k + 1) * 128], ident[0:NPV, 0:NPV]) for k in range(KC)][-1],
              r=["pvrow", "ident"], w=[("ps", 0), ("ps", 1)])
        cp("dve", pvT[l][:].rearrange("p k r -> p (k r)"), bank[:, 0:KC * NPV], [("ps", 0), ("ps", 1)], [("pvT", l)])
        S.add("pe", lambda e: e.transpose(psum[:, 2, 0:2], qkrow[0:2, :], ident[0:2, 0:2]), r=["qkrow", "ident"], w=[("ps", 2)])
        ts("dve", qkT[l][:], psum[:, 2, 0:2], float(np.sqrt(128.0)), None, ALU.mult, None, [("ps", 2)], [("qkT", l)])
        if debug_stage == 1.1:
            return
        mps = 3
        for j2 in range(24):
            src = wmod_d[l, :, j2 * 256:(j2 + 1) * 256].rearrange("(k p) n -> p k n", p=128)
            s = wcount[0] % NSLOT
            wcount[0] += 1
            dstv = wslot[s][:].bitcast(F32).rearrange("p (k n) -> p k n", k=KC)
            S.add("sp", lambda e, dstv=dstv, src=src: e.dma_start(out=dstv, in_=src), r=[], w=[("ws", s)], dma=("ws", s))
            for jj in range(2):
                j = j2 * 2 + jj
                mmgroup(psum[:, mps, 2 * j:2 * j + 2],
                        [(dstv[:, k, jj * 128:(jj + 1) * 128], scT[:, k, :]) for k in range(KC)],
                        r=[("ws", s), "scT"], w=[("ps", mps)])
        if debug_stage == 1.2:
            return
        for sel in range(6):
            tt("dve", modT[l][:, sel, :, :], psum[:, mps, sel * 16:(sel + 1) * 16].rearrange("p (k c) -> p k c", c=2),
               pvT[l][:, :, R_BMOD + sel:R_BMOD + sel + 1].to_broadcast([128, KC, 2]), ALU.add,
               [("ps", mps), ("pvT", l)], [("modT", l)])
        for col in range(2):
            stt("dve", avec[l][:, 0, col, :], modT[l][:, 1, :, col], 1.0, pvT[l][:, :, R_G1], ALU.add, ALU.mult,
                [("modT", l), ("pvT", l)], [("avec", l)])
            stt("dve", avec[l][:, 1, col, :], modT[l][:, 4, :, col], 1.0, pvT[l][:, :, R_G2], ALU.add, ALU.mult,
                [("modT", l), ("pvT", l)], [("avec", l)])
        for i, rr in enumerate((R_GLN, R_BLN, R_BCFC, R_BCFO)):
            cp("dve", lnv[l][:, i, :], pvT[l][:, :, rr], [("pvT", l)], [("lnv", l)])
        if debug_stage == 1.3:
            return
        stg = pa[:, 0:16, :].rearrange("p a b -> p (a b)")
        stgk = [("pa", i) for i in range(16)]
        for j in range(KC):
            for k in range(31):
                ts("dve", stg[:, k * 128:(k + 1) * 128], identb[:], pvT[l][:, j, R_WCF + k:R_WCF + k + 1], None, ALU.mult, None,
                   ["identb", ("pvT", l)], stgk)
            dma("pool", wbf[l, G_DCF + j, :, 0:31 * 128], stg[:, 0:31 * 128], stgk, [("wbf", l, G_DCF + j)], ("dg", l))
        for j in range(KC):
            for k in range(3):
                ts("dve", stg[:, (j * 3 + k) * 128:(j * 3 + k + 1) * 128], identb[:], pvT[l][:, j, R_WSC + k:R_WSC + k + 1], None, ALU.mult, None,
                   ["identb", ("pvT", l)], stgk)
        dma("pool", wbf[l, G_DSC, :, 0:3072], stg[:, 0:3072], stgk, [("wbf", l, G_DSC)], ("dg", l))

    if debug_stage is None or debug_stage >= 1:
        for l in range(n_layers):
            prep(l)

    sqk = [("sq", k) for k in range(KC)]
    b32k = [("big32", k) for k in range(KC)]
    xtk = [("xtok", 0), ("xtok", 1)]

    def headnorm(l, psi, gain_ap, rope, cs_t, cs_key, dst_ap, dst_keys, rope_eng="pool"):
        sqb, sqk = TB16()
        act(sqb[:], psum[:, psi, :], AF.Square, r=[("ps", psi)], w=[sqk])
        HN = int(os.environ.get("HN", "9"))
        if HN < 1:
            return
        p2 = PS()
        mmgroup(psum[:, p2, :], [(onesb[:], sqb[:])], r=["onesb", sqk], w=[("ps", p2)])
        if HN < 2:
            return
        t1, k1 = T32()
        act(t1[:], psum[:, p2, :], AF.Sqrt, r=[("ps", p2), "small"], w=[k1], bias=eps_c[:], scale=1.0)
        rstd, kr = T32()
        S.add("dve", lambda e, rstd=rstd, t1=t1: e.reciprocal(out=rstd[:], in_=t1[:]), r=[k1], w=[kr])
        tg, ktg = T32()
        ts("dve", tg[:], psum[:, psi, :], gain_ap, None, ALU.mult, None, [("ps", psi), ("qkT", l)], [ktg])
        if not rope:
            tt("dve", dst_ap, tg[:], rstd[:], ALU.mult, [ktg, kr], dst_keys)
            return
        qn, kq = TB16()
        tt("dve", qn[:], tg[:], rstd[:], ALU.mult, [ktg, kr], [kq])
        p3 = PS()
        mmgroup(psum[:, p3, :], [(permb[:], qn[:])], r=["permb", kq], w=[("ps", p3)])
        ta, ka = T32()
        tt(rope_eng, ta[:], qn[:], cs_t[:, 0, :], ALU.mult, [kq, cs_key], [ka])
        tb, kb = T32()
        tt("dve", tb[:], psum[:, p3, :], cs_t[:, 1, :], ALU.mult, [("ps", p3), cs_key], [kb])
        tt("dve", dst_ap, ta[:], tb[:], ALU.add, [ka, kb], dst_keys)

    def p1_desc(kind, ti, hi, slot):
        lat = kind == "lat"
        t0 = ti * T
        d = dict(kind=kind, ti=ti, hi=hi, lat=lat, col=0 if lat else 1, t0=t0,
                 kch0=(CTXL // 128 + ti * CPT) if lat else 0, keyoff=(CTXL + t0) if lat else 0,
                 xsrc=x_d if lat else ctx_d, xtd=xTd if lat else cxTd, htd=hTd if lat else chTd,
                 utd=uTd if lat else cuTd, ztd=zTd if lat else czTd, nm="l" if lat else "c",
                 hbuf=hT[hi % 2], hkey=("hT", hi % 2),
                 cst=cs[hi % 2] if lat else None, cskey=("cs", hi % 2) if lat else None,
                 zoff=16 if slot == 0 else 0, uoff=24 if slot == 0 else 8)
        return d

    def p1_pro(l, d):
        xkey = "xT"
        t0, ti, nm, hbuf, hkey, col = d["t0"], d["ti"], d["nm"], d["hbuf"], d["hkey"], d["col"]
        if l == 0:
            for b in range(2):
                dma("act", xtok[:, b, :], d["xsrc"][t0 + b * 128:t0 + (b + 1) * 128, :], [], [("xtok", b)], "xtok")
            for k in range(KC):
                pi = PS("all")
                S.add("pe", lambda e, pi=pi, k=k: [e.transpose(psum[:, pi, b * 128:(b + 1) * 128], xtok[:, b, k * 128:(k + 1) * 128], ident[:]) for b in range(2)][-1],
                      r=xtk + ["ident"], w=[("ps", pi)])
                cp("act" if k % 2 else "dve", xT[:, k, :], psum[:, pi, :], [("ps", pi)], [xkey])
            dma("sp", d["xtd"][ti], xT[:].rearrange("p k t -> p (k t)"), [xkey], [(nm + "xTd", ti)], "xst")
        else:
            dma("act", xT[:].rearrange("p k t -> p (k t)"), d["xtd"][ti], [(nm + "xTd", ti)], [xkey], "xld")
        if d["lat"]:
            dma("act", d["cst"][:, 0, :], cos_d[:, t0:t0 + T], [], [d["cskey"]], d["cskey"])
            dma("act", d["cst"][:, 1, :], sin_d[:, t0:t0 + T], [], [d["cskey"]], d["cskey"])
        act(sq[:].rearrange("p k t -> p (k t)"), xT[:].rearrange("p k t -> p (k t)"), AF.Square, r=[xkey], w=sqk)
        pss = PS("all")
        mmgroup(psum[:, pss, :], [(onesb[:], sq[:, k, :]) for k in range(KC)], r=["onesb"] + sqk, w=[("ps", pss)])
        rstd, kr = rsqrt_from(psum[:, pss, :], [("ps", pss), "small"], 1.0 / D, NORM_EPS)
        tt("dve", big32[:], xT[:], rstd[:].unsqueeze(1).to_broadcast([128, KC, T]), ALU.mult, [xkey, kr], b32k)
        for k in range(KC):
            ts("dve", hbuf[:, k, :], big32[:, k, :], avec[l][:, 0, col, k:k + 1], modT[l][:, 0, k, col:col + 1],
               ALU.mult, ALU.add, [("big32", k), ("avec", l), ("modT", l)], [hkey])
        dma("sp", d["htd"][ti], hbuf[:].rearrange("p k t -> p (k t)"), [hkey], [(nm + "hTd", ti)], "hst")

    def p1_main(l, tiles):
        wkv, kkv = wg(l, G_KV)
        wkv8 = k8(wkv)
        for d in tiles:
            hbuf, hkey = d["hbuf"], d["hkey"]
            for j in range(NKVH):
                pi = PS("all")
                mmgroup(psum[:, pi, :], [(wkv8[:, k, j * 128:(j + 1) * 128], hbuf[:, k, :]) for k in range(KC)],
                        r=[kkv, hkey], w=[("ps", pi)])
                headnorm(l, pi, qkT[l][:, 1:2], d["lat"], d["cst"], d["cskey"], Kres[:, j, d["keyoff"]:d["keyoff"] + T],
                         [("K", d["kch0"] + c) for c in range(CPT)], rope_eng="dve")
            for b in range(CPT):
                pi = PS("all")
                mmgroup(psum[:, pi, :], [(hbuf[:, k, b * 128:(b + 1) * 128], wkv8[:, k, 256:512]) for k in range(KC)],
                        r=[kkv, hkey], w=[("ps", pi)])
                cp("act", Vres[:, d["kch0"] + b, :], psum[:, pi, :], [("ps", pi)], [("V", d["kch0"] + b)])
        for half in range(2):
            wa, ka_ = wg(l, G_CFA + half)
            wgt, kg_ = wg(l, G_CFG + half)
            wa8, wg8 = k8(wa), k8(wgt)
            for d in tiles:
                hbuf, hkey, uo = d["hbuf"], d["hkey"], d["uoff"]
                for oc in range(4):
                    j = half * 4 + oc
                    pa_i = PS("all")
                    pg_i = PS("all")
                    mmgroup(psum[:, pa_i, :], [(wa8[:, k, oc * 128:(oc + 1) * 128], hbuf[:, k, :]) for k in range(KC)], r=[ka_, hkey], w=[("ps", pa_i)])
                    mmgroup(psum[:, pg_i, :], [(wg8[:, k, oc * 128:(oc + 1) * 128], hbuf[:, k, :]) for k in range(KC)], r=[kg_, hkey], w=[("ps", pg_i)])
                    sg, ksg = T32()
                    act(sg[:], psum[:, pg_i, :], AF.Sigmoid, r=[("ps", pg_i)], w=[ksg])
                    tt("dve", pa[:, uo + j, :], psum[:, pa_i, :], sg[:], ALU.mult, [("ps", pa_i), ksg], [("pa", uo + j)])
        for d in tiles:
            uo, t0 = d["uoff"], d["t0"]
            dma("sp", d["utd"][:, :, 15 + t0:15 + t0 + T].rearrange("k p t -> p k t"), pa[:, uo:uo + 8, :],
                [("pa", uo + j) for j in range(KC)], [(d["nm"] + "uTd", d["ti"])], "ust")
        for half in range(2):
            wc, kc_ = wg(l, G_SCC + half)
            wh, kh_ = wg(l, G_SCH + half)
            wc8, wh8 = k8(wc), k8(wh)
            for d in tiles:
                hbuf, hkey, zo = d["hbuf"], d["hkey"], d["zoff"]
                for oc in range(4):
                    j = half * 4 + oc
                    pc_i = PS("all")
                    ph_i = PS("all")
                    mmgroup(psum[:, pc_i, :], [(wc8[:, k, oc * 128:(oc + 1) * 128], hbuf[:, k, :]) for k in range(KC)], r=[kc_, hkey], w=[("ps", pc_i)])
                    mmgroup(psum[:, ph_i, :], [(wh8[:, k, oc * 128:(oc + 1) * 128], hbuf[:, k, :]) for k in range(KC)], r=[kh_, hkey], w=[("ps", ph_i)])
                    cg, kcg = T32()
                    cp("act", cg[:], psum[:, pc_i, :], [("ps", pc_i)], [kcg])
                    tt("dve", pa[:, zo + j, :], psum[:, ph_i, :], cg[:], ALU.mult, [("ps", ph_i), kcg], [("pa", zo + j)])
        for d in tiles:
            zo, t0 = d["zoff"], d["t0"]
            dma("sp", d["ztd"][:, :, 1 + t0:1 + t0 + T].rearrange("k p t -> p k t"), pa[:, zo:zo + 8, :],
                [("pa", zo + j) for j in range(KC)], [(d["nm"] + "zTd", d["ti"])], "zst")

    def p2_loads(l, kind, ti, hi):
        lat = kind == "lat"
        t0 = ti * T
        ntile = NT if lat else 1
        htd = hTd if lat else chTd
        utd = uTd if lat else cuTd
        ztd = zTd if lat else czTd
        nm = "l" if lat else "c"
        hbuf = hT[hi % 2]
        hkey = ("hT", hi % 2)
        dma("act", hbuf[:].rearrange("p k t -> p (k t)"), htd[ti], [(nm + "hTd", ti)], [hkey], hkey)
        if lat:
            cst = cs[hi % 2]
            cskey = ("cs", hi % 2)
            dma("act", cst[:, 0, :], cos_d[:, t0:t0 + T], [], [cskey], cskey)
            dma("act", cst[:, 1, :], sin_d[:, t0:t0 + T], [], [cskey], cskey)
        nbr = [(nm + "uTd", i) for i in (ti - 1, ti, ti + 1) if 0 <= i < ntile] + [(nm + "uTd", "padl"), (nm + "uTd", "padr")]
        dma("act", uwin[:], utd[:, :, t0:t0 + T + 30].rearrange("k p t -> p k t"), nbr, ["uwin"], "uwin")
        nbr = [(nm + "zTd", i) for i in (ti - 1, ti, ti + 1) if 0 <= i < ntile] + [(nm + "zTd", "padl"), (nm + "zTd", "padr")]
        dma("act", zwin[:], ztd[:, :, t0:t0 + T + 2].rearrange("k p t -> p k t"), nbr, ["zwin"], "zwin")

    def p2_xload(kind, ti):
        lat = kind == "lat"
        xtd = xTd if lat else cxTd
        nm = "l" if lat else "c"
        dma("act", xT[:].rearrange("p k t -> p (k t)"), xtd[ti], [(nm + "xTd", ti)], ["xT"], "xld")

    def phase2(l, kind, ti, hi, last, nxt):
        lat = kind == "lat"
        col = 0 if lat else 1
        t0 = ti * T
        xtd = xTd if lat else cxTd
        nm = "l" if lat else "c"
        xkey = "xT"
        hbuf = hT[hi % 2]
        hkey = ("hT", hi % 2)
        if lat:
            cst = cs[hi % 2]
            cskey = ("cs", hi % 2)
        else:
            cst, cskey = None, None
        cur_pool[0] = "gen"
        wqs = {}
        gain_ap = qkT[l][:, 0:1]
        st = {}

        def qs0(h):
            if h % 4 == 0:
                wqs[h // 4] = wg(l, G_Q + h // 4)
            wq, kq_ = wqs[h // 4]
            oc = h % 4
            pi = (h % 4) * 2
            mmgroup(psum[:, pi, :], [(k8(wq)[:, k, oc * 128:(oc + 1) * 128], hbuf[:, k, :]) for k in range(KC)], r=[kq_, hkey], w=[("ps", pi)])
            st[h] = {"pi": pi}

        def qs1(h):
            pi = st[h]["pi"]
            sqb, sqk_ = TB16()
            act(sqb[:], psum[:, pi, :], AF.Square, r=[("ps", pi)], w=[sqk_])
            p2 = 8 + (h % 2) * 2
            mmgroup(psum[:, p2, :], [(onesb[:], sqb[:])], r=["onesb", sqk_], w=[("ps", p2)])
            st[h]["p2"] = p2

        def qs2(h):
            pi, p2 = st[h]["pi"], st[h]["p2"]
            t1, k1 = T32()
            act(t1[:], psum[:, p2, :], AF.Sqrt, r=[("ps", p2), "small"], w=[k1], bias=eps_c[:], scale=1.0)
            rstd, kr = T32()
            S.add("dve", lambda e, rstd=rstd, t1=t1: e.reciprocal(out=rstd[:], in_=t1[:]), r=[k1], w=[kr])
            tg, ktg = T32()
            ts("dve", tg[:], psum[:, pi, :], gain_ap, None, ALU.mult, None, [("ps", pi), ("qkT", l)], [ktg])
            if not lat:
                tt("dve", pa[:, h, :], tg[:], rstd[:], ALU.mult, [ktg, kr], [("pa", h)])
                return
            qn = qnb[h % 4]
            tt("dve", qn[:], tg[:], rstd[:], ALU.mult, [ktg, kr], [("qnb", h % 4)])

        def qs3(h):
            if not lat:
                return
            qn = qnb[h % 4]
            kq = ("qnb", h % 4)
            p3 = 12 + (h % 2) * 2
            mmgroup(psum[:, p3, :], [(permb[:], qn[:])], r=["permb", kq], w=[("ps", p3)])
            ta, ka = T32()
            tt("pool", ta[:], qn[:], cst[:, 0, :], ALU.mult, [kq, cskey], [ka])
            tb, kb = T32()
            tt("dve", tb[:], psum[:, p3, :], cst[:, 1, :], ALU.mult, [("ps", p3), cskey], [kb])
            tt("dve", pa[:, h, :], ta[:], tb[:], ALU.add, [ka, kb], [("pa", h)])

        if os.environ.get("SKEW", "1") == "1":
            for step in range(NQH + 3):
                if step < NQH:
                    qs0(step)
                if 0 <= step - 1 < NQH:
                    qs1(step - 1)
                if 0 <= step - 2 < NQH:
                    qs2(step - 2)
                if 0 <= step - 3 < NQH:
                    qs3(step - 3)
        else:
            for h in range(NQH):
                qs0(h)
                qs1(h)
                qs2(h)
                qs3(h)
        chunks = list(range(NCH)) if lat else list(range(CTXL // 128))
        groups = [chunks[i:i + 4] for i in range(0, len(chunks), 4)]
        ngr = len(groups)
        for j in range(NQH):
            kv = j // (NQH // NKVH)
            po, psm = 8, 10
            qap = pa[:, j, :]

            def emit_S(gi, j=j, kv=kv, qap=qap):
                grp = groups[gi]
                base = (0, 4, 12)[gi % 3]
                for ci, c in enumerate(grp):
                    mmgroup(psum[:, base + ci, :], [(Kres[:, kv, c * 128:(c + 1) * 128], qap)],
                            r=[("K", c), ("pa", j)], w=[("ps", base + ci)])
                n = len(grp)
                act(pt[gi % NPT][:, 0:n, :], psum[:, base:base + n, :], AF.Exp,
                    r=[("ps", base + ci) for ci in range(n)], w=[("pt", gi % NPT)], scale=float(HD ** -0.5))

            def emit_PV(gi, j=j, kv=kv, po=po, psm=psm):
                grp = groups[gi]
                n = len(grp)
                ptb = pt[gi % NPT]
                s1 = ps1b[gi % 2]
                OSUM = os.environ.get("OSUM", "1") == "1"
                if not OSUM:
                    pass
                elif n == 4:
                    s2 = ps2b[gi % 2]
                    tt("dve", s2[:], ptb[:, 0:2, :], ptb[:, 2:4, :], ALU.add, [("pt", gi % NPT)], [("ps2b", gi % 2)])
                    tt("dve", s1[:], s2[:, 0, :], s2[:, 1, :], ALU.add, [("ps2b", gi % 2)], [("ps1b", gi % 2)])
                else:
                    assert n == 2
                    tt("dve", s1[:], ptb[:, 0, :], ptb[:, 1, :], ALU.add, [("pt", gi % NPT)], [("ps1b", gi % 2)])
                for ci, c in enumerate(grp):
                    first = gi == 0 and ci == 0
                    lastc = gi == ngr - 1 and ci == len(grp) - 1
                    mmgroup(psum[:, po, :], [(Vres[:, c, kv * 128:(kv + 1) * 128], ptb[:, ci, :])],
                            r=[("V", c), ("pt", gi % NPT)], w=[("ps", po)], start=first, stop=lastc)
                if OSUM:
                    mmgroup(psum[:, psm, :], [(onesb[:], s1[:])],
                            r=["onesb", ("ps1b", gi % 2)], w=[("ps", psm)], start=(gi == 0), stop=(gi == ngr - 1))
                else:
                    for ci, c in enumerate(grp):
                        mmgroup(psum[:, psm, :], [(onesb[:], ptb[:, ci, :])], r=["onesb", ("pt", gi % NPT)], w=[("ps", psm)],
                                start=(gi == 0 and ci == 0), stop=(gi == ngr - 1 and ci == n - 1))
            emit_S(0)
            if ngr > 1:
                emit_S(1)
            for gi in range(ngr):
                if gi + 2 < ngr:
                    emit_S(gi + 2)
                emit_PV(gi)
            rinv, kri = T32()
            S.add("dve", lambda e, rinv=rinv, psm=psm: e.reciprocal(out=rinv[:], in_=psum[:, psm, :]), r=[("ps", psm)], w=[kri])
            tt("dve", pa[:, 8 + j, :], psum[:, po, :], rinv[:], ALU.mult, [("ps", po), kri], [("pa", 8 + j)])
        cur_pool[0] = "all"
        wdsc, kdsc = wg(l, G_DSC)
        for half in range(2):
            wb_, kb_ = wg(l, G_SCB + half)
            wb8 = k8(wb_)
            for oc in range(4):
                j = half * 4 + oc
                pb = PS()
                mmgroup(psum[:, pb, :], [(wb8[:, k, oc * 128:(oc + 1) * 128], hbuf[:, k, :]) for k in range(KC)], r=[kb_, hkey], w=[("ps", pb)])
                bg, kbg = T32()
                cp("act", bg[:], psum[:, pb, :], [("ps", pb)], [kbg])
                pc = PS()
                mmgroup(psum[:, pc, :], [(wdsc[:, (j * 3 + k) * 128:(j * 3 + k + 1) * 128], zwin[:, j, k:k + T]) for k in range(3)],
                        r=[kdsc, "zwin"], w=[("ps", pc)])
                tt("dve", pa[:, 16 + j, :], psum[:, pc, :], bg[:], ALU.mult, [("ps", pc), kbg], [("pa", 16 + j)])
        for j in range(KC):
            wd, kd = wg(l, G_DCF + j)
            pc = PS()
            mmgroup(psum[:, pc, :], [(wd[:, k * 128:(k + 1) * 128], uwin[:, j, k:k + T]) for k in range(31)],
                    r=[kd, "uwin"], w=[("ps", pc)])
            act(big32[:, j, :], psum[:, pc, :], AF.Identity, r=[("ps", pc), ("lnv", l)], w=[("big32", j)], bias=lnv[l][:, 2, j:j + 1])
            cp("pool", mh[:, j, :], big32[:, j, :], [("big32", j)], [("mh", j)])
            act(sq[:, j, :], big32[:, j, :], AF.Square, r=[("big32", j)], w=[("sq", j)])
        pm = PS()
        pq = PS()
        mmgroup(psum[:, pm, :], [(onesb[:], mh[:, k, :]) for k in range(KC)], r=["onesb"] + [("mh", k) for k in range(KC)], w=[("ps", pm)])
        mmgroup(psum[:, pq, :], [(onesb[:], sq[:, k, :]) for k in range(KC)], r=["onesb"] + [("sq", k) for k in range(KC)], w=[("ps", pq)])
        mean, kmean = T32()
        ts("dve", mean[:], psum[:, pm, :], 1.0 / D, None, ALU.mult, None, [("ps", pm)], [kmean])
        msq, kmsq = T32()
        tt("dve", msq[:], mean[:], mean[:], ALU.mult, [kmean], [kmsq])
        var, kvar = T32()
        stt("dve", var[:], psum[:, pq, :], 1.0 / D, msq[:], ALU.mult, ALU.subtract, [("ps", pq), kmsq], [kvar])
        rstd, kr = rsqrt_from(var[:], [kvar, "small"], 1.0, LN_EPS)
        tt("dve", big32[:], big32[:], mean[:].unsqueeze(1).to_broadcast([128, KC, T]), ALU.subtract, b32k + [kmean], b32k)
        tt("dve", big32[:], big32[:], rstd[:].unsqueeze(1).to_broadcast([128, KC, T]), ALU.mult, b32k + [kr], b32k)
        for j in range(KC):
            act(pa[:, 24 + j, :], big32[:, j, :], AF.Silu, r=[("big32", j), ("lnv", l)], w=[("pa", 24 + j)],
                scale=lnv[l][:, 0, j:j + 1], bias=lnv[l][:, 1, j:j + 1])
        if nxt is not None:
            p2_loads(l, nxt[0], nxt[1], hi + 1)
        for half in range(2):
            for oc in range(4):
                dc = half * 4 + oc
                if oc == 0:
                    wao, kao = wg(l, G_AO + half)
                    wga, kga = wg(l, G_GA + half)
                pya = PS()
                pga = PS()
                mmgroup(psum[:, pya, :], [(k8(wao)[:, k, oc * 128:(oc + 1) * 128], pa[:, 8 + k, :]) for k in range(KC)],
                        r=[kao] + [("pa", 8 + k) for k in range(KC)], w=[("ps", pya)])
                mmgroup(psum[:, pga, :], [(k8(wga)[:, k, oc * 128:(oc + 1) * 128], hbuf[:, k, :]) for k in range(KC)],
                        r=[kga, hkey], w=[("ps", pga)])
                ga, kga_t = T32()
                act(ga[:], psum[:, pga, :], AF.Sigmoid, r=[("ps", pga)], w=[kga_t])
                tt("dve", big32[:, dc, :], psum[:, pya, :], ga[:], ALU.mult, [("ps", pya), kga_t], [("big32", dc)])
        for half in range(2):
            for oc in range(4):
                dc = half * 4 + oc
                if oc == 0:
                    wso, kso = wg(l, G_SO + half)
                    wgb, kgb = wg(l, G_GB + half)
                pys = PS()
                pgb = PS()
                mmgroup(psum[:, pys, :], [(k8(wso)[:, k, oc * 128:(oc + 1) * 128], pa[:, 16 + k, :]) for k in range(KC)],
                        r=[kso] + [("pa", 16 + k) for k in range(KC)], w=[("ps", pys)])
                mmgroup(psum[:, pgb, :], [(k8(wgb)[:, k, oc * 128:(oc + 1) * 128], hbuf[:, k, :]) for k in range(KC)],
                        r=[kgb, hkey], w=[("ps", pgb)])
                gb, kgb_t = T32()
                act(gb[:], psum[:, pgb, :], AF.Sigmoid, r=[("ps", pgb)], w=[kgb_t])
                mb, kmb = T32()
                tt("dve", mb[:], psum[:, pys, :], gb[:], ALU.mult, [("ps", pys), kgb_t], [kmb])
                tt("pool", big32[:, dc, :], big32[:, dc, :], mb[:], ALU.add, [("big32", dc), kmb], [("big32", dc)])
        for half in range(2):
            for oc in range(4):
                dc = half * 4 + oc
                if oc == 0:
                    wco, kco = wg(l, G_CO + half)
                    wgc, kgc = wg(l, G_GC + half)
                pyc = PS()
                pgc = PS()
                mmgroup(psum[:, pyc, :], [(k8(wco)[:, k, oc * 128:(oc + 1) * 128], pa[:, 24 + k, :]) for k in range(KC)],
                        r=[kco] + [("pa", 24 + k) for k in range(KC)], w=[("ps", pyc)])
                mmgroup(psum[:, pgc, :], [(k8(wgc)[:, k, oc * 128:(oc + 1) * 128], hbuf[:, k, :]) for k in range(KC)],
                        r=[kgc, hkey], w=[("ps", pgc)])
                gc, kgc_t = T32()
                act(gc[:], psum[:, pgc, :], AF.Sigmoid, r=[("ps", pgc)], w=[kgc_t])
                mc, kmc = T32()
                stt("dve", mc[:], psum[:, pyc, :], lnv[l][:, 3, dc:dc + 1], gc[:], ALU.add, ALU.mult,
                    [("ps", pyc), kgc_t, ("lnv", l)], [kmc])
                tt("dve", mh[:, dc, :], big32[:, dc, :], mc[:], ALU.add, [("big32", dc), kmc], [("mh", dc)])
        for half in range(2):
            wo, kwo = wg(l, G_WO + half)
            for oc in range(4):
                dc = half * 4 + oc
                po_ = PS()
                mmgroup(psum[:, po_, :], [(k8(wo)[:, k, oc * 128:(oc + 1) * 128], mh[:, k, :]) for k in range(KC)],
                        r=[kwo] + [("mh", k) for k in range(KC)], w=[("ps", po_)])
                stt("dve", xT[:, dc, :], psum[:, po_, :], modT[l][:, 2, dc, col:col + 1], xT[:, dc, :], ALU.mult, ALU.add,
                    [("ps", po_), ("modT", l), xkey], [xkey])
        act(sq[:].rearrange("p k t -> p (k t)"), xT[:].rearrange("p k t -> p (k t)"), AF.Square, r=[xkey], w=sqk)
        pss = PS()
        mmgroup(psum[:, pss, :], [(onesb[:], sq[:, k, :]) for k in range(KC)], r=["onesb"] + [("sq", k) for k in range(KC)], w=[("ps", pss)])
        rstd2, kr2 = rsqrt_from(psum[:, pss, :], [("ps", pss), "small"], 1.0 / D, NORM_EPS)
        tt("dve", big32[:], xT[:], rstd2[:].unsqueeze(1).to_broadcast([128, KC, T]), ALU.mult, [xkey, kr2], b32k)
        for k in range(KC):
            ts("dve", mh[:, k, :], big32[:, k, :], avec[l][:, 1, col, k:k + 1], modT[l][:, 3, k, col:col + 1],
               ALU.mult, ALU.add, [("big32", k), ("avec", l), ("modT", l)], [("mh", k)])
        mhk = [("mh", k) for k in range(KC)]
        for g in range(8):
            w1, kw1 = wg(l, G_M1 + g)
            w18 = k8(w1)
            for oc in range(4):
                fc = g * 4 + oc
                ph = PS()
                mmgroup(psum[:, ph, :], [(w18[:, k, oc * 128:(oc + 1) * 128], mh[:, k, :]) for k in range(KC)], r=[kw1] + mhk, w=[("ps", ph)])
                rl, krl = TB16()
                act(rl[:], psum[:, ph, :], AF.Relu, r=[("ps", ph)], w=[krl])
                tt("pool", pa[:, fc, :], rl[:], rl[:], ALU.mult, [krl], [("pa", fc)])
        pak = [("pa", i) for i in range(32)]
        for dc in range(KC):
            w2, kw2 = wg(l, G_M2 + dc)
            w2v = w2.rearrange("p (k n) -> p k n", k=FC)
            po_ = PS()
            mmgroup(psum[:, po_, :], [(w2v[:, f, :], pa[:, f, :]) for f in range(FC)], r=[kw2] + pak, w=[("ps", po_)])
            stt("dve", xT[:, dc, :], psum[:, po_, :], modT[l][:, 5, dc, col:col + 1], xT[:, dc, :], ALU.mult, ALU.add,
                [("ps", po_), ("modT", l), xkey], [xkey])
        if not (last and lat):
            dma("sp", xtd[ti], xT[:].rearrange("p k t -> p (k t)"), [xkey], [(nm + "xTd", ti)], "xst")
        else:
            for b in range(CPT):
                bankA = psum[:, 12:14, :].rearrange("p a b -> p (a b)")
                bankB = psum[:, 14:16, :].rearrange("p a b -> p (a b)")
                S.add("pe", lambda e, b=b: [e.transpose((bankA if k < 4 else bankB)[:, (k % 4) * 128:(k % 4 + 1) * 128], xT[:, k, b * 128:(b + 1) * 128], ident[:]) for k in range(KC)][-1],
                      r=[xkey, "ident"], w=[("ps", i) for i in (12, 13, 14, 15)])
                full = psum[:, 12:16, :].rearrange("p a b -> p (a b)")
                junk = big32[:].rearrange("p k t -> p (k t)")[:, 0:D]
                ssb, kss = T32()
                S.add("dve", lambda e, ssb=ssb: e.memset(ssb[:, 0:1], 0.0), w=[kss])
                act(junk, full, AF.Square, r=[("ps", i) for i in (12, 13, 14, 15)], w=b32k[0:4] + [kss], accum_out=ssb[:, 0:1])
                l1, kl1 = T32()
                act(l1[:, 0:1], ssb[:, 0:1], AF.Sqrt, r=[kss, "small"], w=[kl1], scale=1.0 / D, bias=eps_a[:])
                rs, krs = T32()
                S.add("dve", lambda e, rs=rs, l1=l1: e.reciprocal(out=rs[:, 0:1], in_=l1[:, 0:1]), r=[kl1], w=[krs])
                stt("dve", xtok[:, b, :], full, rs[:, 0:1], gfin[:], ALU.mult, ALU.mult,
                    [("ps", i) for i in (12, 13, 14, 15)] + [krs, "gfin"], [("xtok", b)])
                dma("sp", out_d[t0 + b * 128:t0 + (b + 1) * 128, :], xtok[:, b, :], [("xtok", b)], [("out", ti, b)], "outst")
        if nxt is not None:
            p2_xload(nxt[0], nxt[1])

    for l in range(n_layers):
        if debug_stage is not None and debug_stage < 2:
            break
        last = l == n_layers - 1
        seq1 = [("ctx", 0)] + [("lat", ti) for ti in range(NT)]
        for i in range(0, len(seq1), 2):
            tiles = [p1_desc(kd, ti, i + sl, sl) for sl, (kd, ti) in enumerate(seq1[i:i + 2])]
            for d in tiles:
                p1_pro(l, d)
            p1_main(l, tiles)
        if debug_stage == 3:
            break
        cur_pool[0] = "all"
        seq = [("lat", ti) for ti in range(NT)] + ([] if last else [("ctx", 0)])
        p2_loads(l, seq[0][0], seq[0][1], 0)
        p2_xload(seq[0][0], seq[0][1])
        for i, (kd, ti) in enumerate(seq):
            phase2(l, kd, ti, i, last, seq[i + 1] if i + 1 < len(seq) else None)
        cur_pool[0] = "all"

    S.finalize()

    if os.environ.get("KDUMP"):
        for e in Sched.ENGS:
            print("ENGINE", e, len(S.q[e]))
            for i, op in enumerate(S.q[e][-int(os.environ["KDUMP"]):]):
                print("  ", i, "dma" if op.is_dma else "", op.dkey, "sig" if op.sig else "", op.cnt, op.waits)
    if os.environ.get("KSIM"):
        semv = {}
        pos = {e: 0 for e in Sched.ENGS}
        progress = True
        while progress:
            progress = False
            for e in Sched.ENGS:
                while pos[e] < len(S.q[e]):
                    op = S.q[e][pos[e]]
                    if all(semv.get(k, 0) >= v for k, v in op.waits):
                        if op.is_dma:
                            semv[("d", op.dkey)] = semv.get(("d", op.dkey), 0) + 16
                        elif op.sig:
                            k = ("e", op.eng, (op.cnt - 1) // EPOCH)
                            semv[k] = semv.get(k, 0) + 1
                        pos[e] += 1
                        progress = True
                    else:
                        break
        for e in Sched.ENGS:
            print("SIM", e, pos[e], "/", len(S.q[e]))
            if pos[e] < len(S.q[e]):
                op = S.q[e][pos[e]]
                print("   stuck on waits", [(k, v, semv.get(k, 0)) for k, v in op.waits])
    sems = {}

    def getsem(key):
        s = sems.get(key)
        if s is None:
            s = es.enter_context(nc.semaphore("s%d" % len(sems)))
            sems[key] = s
        return s

    for e in Sched.ENGS:
        for op in S.q[e]:
            for key, val in op.waits:
                getsem(key)
            if op.is_dma:
                getsem(("d", op.dkey))
            elif op.sig:
                getsem(("e", op.eng, (op.cnt - 1) // EPOCH))


    if os.environ.get("KDUMP"):
        print("NSEMS", len(sems), [ (k, getattr(v, "num", None)) for k, v in sems.items()])

    def emit(eng_obj, ops, final_keys=(), throttle=0):
        issued = {}
        for op in ops:
            for key, val in op.waits:
                eng_obj.wait_ge(sems[key], val)
            if throttle and op.is_dma:
                n = issued.get(op.dkey, 0)
                if n > 0 and n % throttle == 0:
                    eng_obj.wait_ge(sems[("d", op.dkey)], 16 * n)
                issued[op.dkey] = n + 1
            ins = op.fn(eng_obj)
            if op.is_dma:
                ins.then_inc(sems[("d", op.dkey)], 16)
            elif op.sig:
                ins.then_inc(sems[("e", op.eng, (op.cnt - 1) // EPOCH)], 1)
        for k in final_keys:
            if k in S.dmacnt:
                eng_obj.wait_ge(sems[("d", k)], 16 * S.dmacnt[k])

    with nc.Block() as block:
        @block.sync
        def _(e):
            emit(e, S.q["sp"])

        @block.tensor
        def _(e):
            emit(e, S.q["pe"])

        @block.scalar
        def _(e):
            emit(e, S.q["act"])

        @block.vector
        def _(e):
            emit(e, S.q["dve"])

        @block.gpsimd
        def _(e):
            emit(e, S.q["pool"], final_keys=list(S.dmacnt.keys()), throttle=int(os.environ.get("THR", "2")))
    es.close()
    return nc


def rope_tables(S_len):
    GRID_W = 64
    t = np.arange(S_len)
    pos = np.stack([t // GRID_W, t % GRID_W], axis=0).astype(np.float32)
    inv_freq = (10000.0 ** (-np.arange(32, dtype=np.float32) * 2.0 / 64.0)).astype(np.float32)
    ang = pos[:, None, :] * inv_freq[None, :, None]
    cos = np.cos(ang).astype(np.float32)
    sin = np.sin(ang).astype(np.float32)
    cosT = np.zeros((128, S_len), np.float32)
    sinT = np.zeros((128, S_len), np.float32)
    for ax in range(2):
        for half in range(2):
            p0 = ax * 64 + half * 32
            cosT[p0:p0 + 32] = cos[ax]
            sinT[p0:p0 + 32] = sin[ax] * (-1.0 if half == 0 else 1.0)
    perm = np.zeros((128, 128), np.float32)
    for m in range(128):
        partner = m + 32 if (m % 64) < 32 else m - 32
        perm[partner, m] = 1.0
    return cosT, sinT, perm


def make_in_maps(inp, S_len, nb):
    f = lambda a: np.ascontiguousarray(np.asarray(a, dtype=np.float32))
    cosT, sinT, perm = rope_tables(S_len)
    pv = np.zeros((L, NPV, D), np.float32)
    pv[:, R_G1] = inp["g_norm1"]
    pv[:, R_G2] = inp["g_norm2"]
    pv[:, R_BCFC] = inp["b_cf_conv"]
    pv[:, R_GLN] = inp["g_cf_ln"]
    pv[:, R_BLN] = inp["b_cf_ln"]
    pv[:, R_BCFO] = inp["b_cf_out"]
    pv[:, R_BMOD:R_BMOD + 6] = np.asarray(inp["b_mod"]).reshape(L, 6, D)
    pv[:, R_WSC:R_WSC + 3] = inp["w_sc_conv"]
    pv[:, R_WCF:R_WCF + 31] = inp["w_cf_conv"]
    qk = np.stack([np.asarray(inp["q_gain"]), np.asarray(inp["k_gain"])], axis=1).astype(np.float32)
    shared = {
        "w_mod": f(inp["w_mod"]), "w_in": f(inp["w_in"]), "w_attn_out": f(inp["w_attn_out"]),
        "w_sc_out": f(inp["w_sc_out"]), "w_cf_out": f(inp["w_cf_out"]), "w_o": f(inp["w_o"]),
        "w_mlp_in": f(inp["w_mlp_in"]), "w_mlp_out": f(inp["w_mlp_out"]), "pv": pv, "qk": f(qk),
        "g_final": f(inp["g_final"]), "rope_cos": cosT, "rope_sin": sinT,
        "ident": np.eye(128, dtype=np.float32), "perm": perm,
    }
    x = np.asarray(inp["x"]); c = np.asarray(inp["c"]); ctx = np.asarray(inp["ctx"]); c_ctx = np.asarray(inp["c_ctx"])
    maps = []
    for b in range(nb):
        m = dict(shared)
        m["x"] = f(x[b])
        m["ctx"] = f(ctx[b])
        m["cc"] = f(np.stack([c[b], c_ctx], axis=0))
        maps.append(m)
    return maps


def kernel(**inputs):
    x = np.asarray(inputs["x"])
    B, S_len, _ = x.shape
    nc = build(S_len)
    in_maps = make_in_maps(inputs, S_len, B)
    res = run_bass_kernel_spmd(nc, in_maps, core_ids=list(range(B)))
    return np.stack([np.asarray(r["out"]) for r in res.results], axis=0).astype(np.float32)
```
